# Optimizing a Trainium2 kernel written in Bass

```python
import math
import jax, jax.numpy as jnp
from jax import lax
import numpy as np

D_MODEL = 1024
BATCH = 32
SEQ = 2048
DEPTH = 2

N_MIXERS = 2
HEAD_DIM = 64
N_HEADS = D_MODEL // HEAD_DIM
Q_BLOCK = 128
CONV_WIDTH = 31
CONV_INNER = D_MODEL
FFN_DIM = 2816
FFN_CONV_WIDTH = 3
EPS = 1e-6
N_A = (DEPTH + 1) // 2
N_B = DEPTH // 2

kernel_name = "fox_conformer_convffn_hybrid"


def rms_norm(x, g):
    xf = x.astype(jnp.float32)
    y = xf * lax.rsqrt(jnp.mean(xf * xf, axis=-1, keepdims=True) + EPS)
    return (y * g.astype(jnp.float32)).astype(x.dtype)


def causal_dwconv(x, w):
    k, c = w.shape
    return lax.conv_general_dilated(
        x, w[:, None, :].astype(x.dtype), window_strides=(1,), padding=[(k - 1, 0)],
        dimension_numbers=("NWC", "WIO", "NWC"), feature_group_count=c)


def fox_attention(h, w_in, b_f, q_g, k_g, w_o):
    b, s, d = h.shape
    proj = h @ w_in
    q = proj[..., :d].reshape(b, s, N_HEADS, HEAD_DIM)
    k = proj[..., d:2 * d].reshape(b, s, N_HEADS, HEAD_DIM)
    v = proj[..., 2 * d:3 * d].reshape(b, s, N_HEADS, HEAD_DIM)
    f_logit = (proj[..., 3 * d:] + b_f).astype(jnp.float32)
    log_f = jax.nn.log_sigmoid(f_logit)
    cum = jnp.transpose(jnp.cumsum(log_f, axis=1), (0, 2, 1))

    q = rms_norm(q, q_g).astype(jnp.float32)
    k = rms_norm(k, k_g).astype(jnp.float32)
    q = jnp.transpose(q, (0, 2, 1, 3))
    k = jnp.transpose(k, (0, 2, 1, 3))
    v = jnp.transpose(v, (0, 2, 1, 3)).astype(jnp.float32)
    scale = 1.0 / math.sqrt(HEAD_DIM)

    outs = []
    for i in range(s // Q_BLOCK):
        q0, q1 = i * Q_BLOCK, (i + 1) * Q_BLOCK
        qb = q[:, :, q0:q1]
        kb, vb = k[:, :, :q1], v[:, :, :q1]
        logits = jnp.einsum("bhqd,bhkd->bhqk", qb, kb) * scale
        logits = logits + (cum[:, :, q0:q1, None] - cum[:, :, None, :q1])
        qpos = jnp.arange(q0, q1)[:, None]
        kpos = jnp.arange(q1)[None, :]
        logits = jnp.where(kpos <= qpos, logits, -jnp.inf)
        p = jax.nn.softmax(logits, axis=-1)
        outs.append(jnp.einsum("bhqk,bhkd->bhqd", p, vb))
    o = jnp.concatenate(outs, axis=2)
    o = jnp.transpose(o, (0, 2, 1, 3)).reshape(b, s, d).astype(h.dtype)
    return o @ w_o


def conformer_conv(h, w_pw1, b_pw1, w_dw, b_dw, ln_g, w_pw2, b_pw2):
    u = h @ w_pw1 + b_pw1
    a, g = jnp.split(u, 2, axis=-1)
    u = a * jax.nn.sigmoid(g)
    u = causal_dwconv(u, w_dw) + b_dw
    u = rms_norm(u, ln_g)
    u = jax.nn.silu(u)
    return u @ w_pw2 + b_pw2


def conv_ffn(h, w_up, w_dw, b_dw, w_down):
    u = h @ w_up
    u = causal_dwconv(u, w_dw) + b_dw
    gate, val = jnp.split(u, 2, axis=-1)
    return (jax.nn.silu(gate) * val) @ w_down


def setup_inputs(seed: int = 0) -> dict:
    key = jax.random.key(seed)
    ks = iter(jax.random.split(key, 32))
    D, H, C, F = D_MODEL, N_HEADS, CONV_INNER, FFN_DIM
    out_scale = (2 * DEPTH) ** -0.5

    def nrm(shape, scale):
        return jax.random.normal(next(ks), shape, jnp.float32) * scale

    def gain(shape):
        return 1.0 + nrm(shape, 0.02)

    x = jax.random.normal(next(ks), (BATCH, SEQ, D), jnp.float32)
    fg_base = jnp.linspace(0.0, 4.0, H, dtype=jnp.float32)
    return {
        "x": x,
        "fox_norm_g": gain((N_A, D)),
        "fox_w_in": nrm((N_A, D, 3 * D + H), D ** -0.5),
        "fox_b_f": fg_base[None, :] + nrm((N_A, H), 0.1),
        "fox_q_g": gain((N_A, H, HEAD_DIM)),
        "fox_k_g": gain((N_A, H, HEAD_DIM)),
        "fox_w_o": nrm((N_A, D, D), D ** -0.5 * out_scale),
        "conv_norm_g": gain((N_B, D)),
        "conv_w_pw1": nrm((N_B, D, 2 * C), D ** -0.5),
        "conv_b_pw1": nrm((N_B, 2 * C), 0.02),
        "conv_w_dw": nrm((N_B, CONV_WIDTH, C), CONV_WIDTH ** -0.5),
        "conv_b_dw": nrm((N_B, C), 0.02),
        "conv_ln_g": gain((N_B, C)),
        "conv_w_pw2": nrm((N_B, C, D), C ** -0.5 * out_scale),
        "conv_b_pw2": nrm((N_B, D), 0.02),
        "ffn_norm_g": gain((DEPTH, D)),
        "ffn_w_up": nrm((DEPTH, D, 2 * F), D ** -0.5),
        "ffn_w_dw": nrm((DEPTH, FFN_CONV_WIDTH, 2 * F), FFN_CONV_WIDTH ** -0.5),
        "ffn_b_dw": nrm((DEPTH, 2 * F), 0.02),
        "ffn_w_down": nrm((DEPTH, F, D), F ** -0.5 * out_scale),
    }


def reference(x, fox_norm_g, fox_w_in, fox_b_f, fox_q_g, fox_k_g, fox_w_o,
              conv_norm_g, conv_w_pw1, conv_b_pw1, conv_w_dw, conv_b_dw, conv_ln_g,
              conv_w_pw2, conv_b_pw2,
              ffn_norm_g, ffn_w_up, ffn_w_dw, ffn_b_dw, ffn_w_down):
    for i in range(DEPTH):
        j = i // N_MIXERS
        if i % N_MIXERS == 0:
            h = rms_norm(x, fox_norm_g[j])
            x = x + fox_attention(h, fox_w_in[j], fox_b_f[j], fox_q_g[j], fox_k_g[j], fox_w_o[j])
        else:
            h = rms_norm(x, conv_norm_g[j])
            x = x + conformer_conv(h, conv_w_pw1[j], conv_b_pw1[j], conv_w_dw[j], conv_b_dw[j],
                                   conv_ln_g[j], conv_w_pw2[j], conv_b_pw2[j])
        h = rms_norm(x, ffn_norm_g[i])
        x = x + conv_ffn(h, ffn_w_up[i], ffn_w_dw[i], ffn_b_dw[i], ffn_w_down[i])
    return x
```

```python
import numpy as np
import concourse.bass as bass
import concourse.mybir as mybir
from concourse.bass_utils import run_bass_kernel_spmd

F32 = mybir.dt.float32
BF16 = mybir.dt.bfloat16
AF = mybir.ActivationFunctionType
ALU = mybir.AluOpType

D = 1024
S = 2048
H = 16
DH = 64
FF = 2816
NFC = FF // 128
CW = 31
EPS = 1e-6
NCH = D // 128
TT = 512
NT = S // TT
NTT = S // 128
NCORES = 8
LIMIT = 30000
ENGS = ("pe", "act", "dve", "pool", "sp")
KB = 1024
NEG = -30000.0


class Buf:
    __slots__ = ("w", "r")

    def __init__(self):
        self.w = None
        self.r = {}


class Plan:
    def __init__(self):
        self.streams = {e: [] for e in ENGS}
        self.count = {e: 0 for e in ENGS}
        self.seen = {e: {} for e in ENGS}
        self.dma_count = {}

    def emit(self, eng, fn, reads=(), writes=(), dma=None):
        need = {}
        own = ("e", eng)
        for b in reads:
            if b.w is not None:
                k, v = b.w
                if need.get(k, 0) < v:
                    need[k] = v
        for b in writes:
            if b.w is not None:
                k, v = b.w
                if need.get(k, 0) < v:
                    need[k] = v
            for k, v in b.r.items():
                if k == own:
                    continue
                if need.get(k, 0) < v:
                    need[k] = v
        waits = []
        seen = self.seen[eng]
        for k, v in need.items():
            if k == own and eng == "pe":
                continue
            if seen.get(k, 0) >= v:
                continue
            seen[k] = v
            waits.append((k, v))
        if dma is None:
            self.count[eng] += 1
            tok = (own, self.count[eng])
        else:
            self.dma_count[dma] = self.dma_count.get(dma, 0) + 16
            tok = (("d", dma), self.dma_count[dma])
        self.streams[eng].append((waits, fn, tok))
        for b in reads:
            b.r[tok[0]] = tok[1]
        for b in writes:
            b.w = tok
            b.r = {}
        return tok

    def all_tokens(self):
        d = {("e", e): c for e, c in self.count.items() if c > 0}
        for k, v in self.dma_count.items():
            d[("d", k)] = v
        return d

    def fresh(self, n=None):
        def mk():
            b = Buf()
            b.r = self.all_tokens()
            return b
        if n is None:
            return mk()
        return [mk() for _ in range(n)]


def build_program(NSEQ, n_phases=99):
    nc = bass.Bass("TRN2", target_bir_lowering=False)
    P = Plan()

    def din(name, shape):
        return nc.dram_tensor(name, list(shape), F32, kind="ExternalInput").ap()

    x_d = din("x", [NSEQ, S, D])
    y_d = nc.dram_tensor("y", [NSEQ, S, D], F32, kind="ExternalOutput").ap()
    pvec_d = din("pvec", [42, D])
    p2_d = din("p2", [8, 2 * FF])
    bf_d = din("b_f", [H, 1])
    w_in_d = din("fox_w_in", [D, 3 * D + H])
    w_o_d = din("fox_w_o", [D, D])
    pw1_d = din("conv_w_pw1", [D, 2 * D])
    pw2_d = din("conv_w_pw2", [D, D])
    wup_d = [din("ffn_w_up%d" % i, [D, 2 * FF]) for i in range(2)]
    wdn_d = [din("ffn_w_down%d" % i, [FF, D]) for i in range(2)]

    def kview(w):
        return w.rearrange("(kc p) n -> p kc n", p=128)

    def sb(name, shape, dt, off):
        return nc.alloc_sbuf_tensor_at(name, list(shape), dt, offset=off)

    o = 16640
    ident_f = sb("ident_f", [128, 128], F32, o); o += 512
    ident_b = sb("ident_b", [128, 128], BF16, o); o += 256
    ones_b = sb("ones_b", [128, 128], BF16, o); o += 256
    blk_b = sb("blk_b", [128, 128], BF16, o); o += 256
    maskneg = sb("maskneg", [128, 128], F32, o); o += 512
    cols1 = sb("cols1", [128, NCH, 42], F32, o); o += NCH * 42 * 4
    cols2 = sb("cols2", [128, 2 * NFC, 8], F32, o); o += 2 * NFC * 8 * 4
    negbf = sb("negbf", [H, 1], F32, o); o += 32
    gqs = sb("gqs", [128, NCH], F32, o); o += 32
    zeros_f = sb("zeros_f", [128, 128], F32, o); o += 512
    ones16 = sb("ones16", [H, TT], F32, o); o += TT * 4
    CONST_END = (o + 63) // 64 * 64
    BASE = CONST_END
    ARENA = 16640 + 207 * KB - BASE

    b_const = Buf()

    banks = [nc.alloc_psum_tensor("bank%d" % i, [128, 512], F32) for i in range(8)]
    bank_buf = [Buf() for _ in range(8)]

    class Pool_:
        def __init__(self, ids):
            self.ids = ids
            self.i = 0

        def next(self):
            b = self.ids[self.i % len(self.ids)]
            self.i += 1
            return banks[b], bank_buf[b]

    ps_mm = Pool_([0, 1, 2, 3])
    ps_s = Pool_([0, 1, 2])
    ps_aux = Pool_([3, 4, 5])
    ps_o = Pool_([6, 7])
    ps_up = Pool_([0, 1, 2, 3, 4, 5])

    E = P.emit

    stage1 = sb("stage1", [42, D], F32, BASE)
    stage2 = sb("stage2", [8, 2 * FF], F32, BASE + D * 4)
    b_stage1 = Buf()
    b_stage2 = Buf()
    E("sp", lambda e: e.dma_start(out=stage1[:, :], in_=pvec_d[:, :]), writes=[b_stage1], dma="cst0")
    E("sp", lambda e: e.dma_start(out=stage2[:, :], in_=p2_d[:, :]), writes=[b_stage2], dma="cst1")
    E("sp", lambda e: e.dma_start(out=negbf[:, :], in_=bf_d[:, :]), writes=[b_const], dma="cst2")
    E("pool", lambda e: e.memset(ones_b[:, :], 1.0), writes=[b_const])
    E("pool", lambda e: e.memset(zeros_f[:, :], 0.0), writes=[b_const])
    E("pool", lambda e: e.memset(ones16[:, :], 1.0), writes=[b_const])
    E("pool", lambda e: e.memset(ident_f[:, :], 1.0), writes=[b_const])
    E("pool", lambda e: e.affine_select(out=ident_f[:, :], in_=ident_f[:, :], pattern=[[-1, 128]],
                                        compare_op=ALU.is_equal, fill=0.0, base=0, channel_multiplier=1),
      reads=[b_const], writes=[b_const])
    E("pool", lambda e: e.affine_select(out=maskneg[:, :], in_=zeros_f[:, :], pattern=[[1, 128]],
                                        compare_op=ALU.is_ge, fill=NEG, base=0, channel_multiplier=-1),
      reads=[b_const], writes=[b_const])
    E("pool", lambda e: e.tensor_copy(out=ident_b[:, :], in_=ident_f[:, :]), reads=[b_const], writes=[b_const])
    E("pool", lambda e: e.memset(blk_b[:, :], 0.0), writes=[b_const])
    E("pool", lambda e: e.memset(blk_b[0:64, 0:64], 1.0), reads=[b_const], writes=[b_const])
    E("pool", lambda e: e.memset(blk_b[64:128, 64:128], 1.0), reads=[b_const], writes=[b_const])
    E("dve", lambda e: e.tensor_scalar(out=negbf[:, :], in0=negbf[:, :], scalar1=-1.0, scalar2=None, op0=ALU.mult),
      reads=[b_const], writes=[b_const])
    for c in range(NCH):
        bk, bb = ps_aux.next()
        E("pe", lambda e, c=c, bk=bk: e.transpose(out=bk[:, 0:42], in_=stage1[0:42, c * 128:(c + 1) * 128],
                                                  identity=ident_f[0:42, 0:42]),
          reads=[b_stage1, b_const], writes=[bb])
        E("dve", lambda e, c=c, bk=bk: e.tensor_copy(out=cols1[:, c, :], in_=bk[:, 0:42]), reads=[bb], writes=[b_const])
    for c in range(2 * NFC):
        bk, bb = ps_aux.next()
        E("pe", lambda e, c=c, bk=bk: e.transpose(out=bk[:, 0:8], in_=stage2[0:8, c * 128:(c + 1) * 128],
                                                  identity=ident_f[0:8, 0:8]),
          reads=[b_stage2, b_const], writes=[bb])
        E("dve", lambda e, c=c, bk=bk: e.tensor_copy(out=cols2[:, c, :], in_=bk[:, 0:8]), reads=[bb], writes=[b_const])
    E("dve", lambda e: e.tensor_scalar(out=gqs[:, :], in0=cols1[:, :, 1], scalar1=0.125, scalar2=None, op0=ALU.mult),
      reads=[b_const], writes=[b_const])

    def col1(c, r):
        return cols1[:, c, r:r + 1]

    def norm_tile(src, src_bufs, gcol, dst, dst_bufs, N, nrm):
        sq, sq_b, lnv, lnv_b, rstd, rstd_b = nrm
        ssb, ssbb = ps_aux.next()
        for c in range(NCH):
            i = c % 2
            E("act", lambda e, c=c, i=i: e.activation(out=sq[i][:, 0:N], in_=src(c), func=AF.Square),
              reads=[src_bufs[c]], writes=[sq_b[i]])
            E("pe", lambda e, c=c, i=i: e.matmul(ssb[:, 0:N], lhsT=ones_b[:, :], rhs=sq[i][:, 0:N],
                                                 start=(c == 0), stop=(c == NCH - 1)),
              reads=[sq_b[i], b_const], writes=[ssbb])
        E("act", lambda e: e.activation(out=lnv[:, 0:N], in_=ssb[:, 0:N], func=AF.Ln, bias=EPS, scale=1.0 / D),
          reads=[ssbb], writes=[lnv_b])
        E("act", lambda e: e.activation(out=rstd[:, 0:N], in_=lnv[:, 0:N], func=AF.Exp, scale=-0.5),
          reads=[lnv_b], writes=[rstd_b])
        for c in range(NCH):
            E("dve", lambda e, c=c: e.scalar_tensor_tensor(out=dst(c), in0=src(c), scalar=gcol(c), in1=rstd[:, 0:N],
                                                           op0=ALU.mult, op1=ALU.mult),
              reads=[src_bufs[c], rstd_b, b_const], writes=[dst_bufs[c]])

    def nrm_alloc(off):
        sq = [sb("sq%d_%d" % (i, off), [128, TT], BF16, off + i * TT * 2) for i in range(2)]
        lnv = sb("lnv_%d" % off, [128, TT], F32, off + 2 * KB)
        rstd = sb("rstd_%d" % off, [128, TT], F32, off + 4 * KB)
        return (sq, P.fresh(2), lnv, P.fresh(), rstd, P.fresh()), 6 * KB

    def load_x_T(b, tt, xin, xin_b, dst_fn, dst_bufs):
        i = tt % len(xin)
        E("sp", lambda e: e.dma_start(out=xin[i][:, :], in_=x_d[b, tt * 128:(tt + 1) * 128, :]),
          writes=[xin_b[i]], dma="xin%d" % i)
        for hh in range(2):
            bk, bb = ps_aux.next()
            for cc in range(4):
                c = hh * 4 + cc
                E("pe", lambda e, c=c, cc=cc, bk=bk: e.transpose(out=bk[:, cc * 128:(cc + 1) * 128],
                                                                 in_=xin[i][:, c * 128:(c + 1) * 128],
                                                                 identity=ident_f[:, :]),
                  reads=[xin_b[i], b_const], writes=[bb])
            E("dve", lambda e, hh=hh, bk=bk: e.tensor_copy(
                out=dst_fn(hh), in_=bk[:, :].rearrange("p (c t) -> p c t", c=4)),
              reads=[bb], writes=dst_bufs)

    def load_w(dst_ap, src_ap, buf, sem):
        E("pool", lambda e: e.dma_start(out=dst_ap, in_=src_ap), writes=[buf], dma=sem)

    def seq_body(b):
        o = BASE
        hT = sb("hT_f%d" % b, [128, NCH, S], BF16, o); o += 32 * KB
        oT = sb("oT_f%d" % b, [128, NCH, S], BF16, o); o += 32 * KB
        qa = [sb("qa%d_%d" % (i, b), [70, 2, S], BF16, o + i * 8 * KB) for i in range(2)]; o += 16 * KB
        ka = [sb("ka%d_%d" % (i, b), [70, 2, S], BF16, o + i * 8 * KB) for i in range(2)]; o += 16 * KB
        va = [sb("va%d_%d" % (i, b), [128, NTT, 192], BF16, o + i * 6 * KB) for i in range(2)]; o += 12 * KB
        xtmp = sb("xtmp%d" % b, [128, NCH, TT], F32, o); o += 16 * KB
        qs_full = sb("qsf%d" % b, [H, 3, S], BF16, o); o += 12 * KB
        cumt = [sb("cumt%d_%d" % (i, b), [H, TT], F32, o + i * 2 * KB) for i in range(2)]; o += 4 * KB
        et = sb("et%d" % b, [H, TT], F32, o); o += 2 * KB
        lt = sb("lt%d" % b, [H, TT], F32, o); o += 2 * KB
        r1 = sb("r1_%d" % b, [H, TT], F32, o); o += 2 * KB
        xin = [sb("xin%d_%d" % (i, b), [128, D], F32, o + i * 4 * KB) for i in range(4)]; o += 16 * KB
        PT = [sb("PT%d_%d" % (i, b), [128, TT], BF16, o + i * KB) for i in range(4)]; o += 4 * KB
        nrm, sz = nrm_alloc(o); o += sz
        rec = [sb("rec%d_%d" % (i, b), [128, TT], F32, o + i * 2 * KB) for i in range(2)]; o += 4 * KB
        wq = [sb("wq%d_%d" % (i, b), [128, NCH, 384], BF16, o + i * 6 * KB) for i in range(2)]; o += 12 * KB
        wf = sb("wf%d" % b, [128, NCH, H], BF16, o); o += 256
        assert o <= BASE + ARENA, ("fox", o)

        hT_b = [[P.fresh() for _ in range(NT)] for _ in range(NCH)]
        oT_b = [[P.fresh() for _ in range(NT)] for _ in range(NCH)]
        qa_b = P.fresh(2); ka_b = P.fresh(2); va_b = P.fresh(2)
        qaaug_b = P.fresh(2); kaaug_b = P.fresh(2)
        xtmp_b = P.fresh(NCH)
        qs_b = P.fresh(); cum_b = P.fresh(2); et_b = P.fresh(); lt_b = P.fresh(); r1_b = P.fresh()
        xin_b = P.fresh(4); PT_b = P.fresh(4); rec_b = P.fresh(2)
        wq_b = P.fresh(2); wf_b = P.fresh()

        for i in range(2):
            E("pool", lambda e, i=i: e.memset(qa[i][64:70, :, :], -1.0), writes=[qaaug_b[i]])
            E("pool", lambda e, i=i: e.memset(ka[i][64:70, :, :], 1.0), writes=[kaaug_b[i]])
            E("pool", lambda e, i=i: e.memset(va[i][:, :, 64:128], 1.0), writes=[va_b[i]])

        for t in range(NT):
            for j in range(4):
                tt = 4 * t + j
                load_x_T(b, tt, xin, xin_b,
                         lambda hh, j=j: xtmp[:, 4 * hh:4 * hh + 4, j * 128:(j + 1) * 128], xtmp_b)
            norm_tile(lambda c: xtmp[:, c, :], xtmp_b, lambda c: col1(c, 0),
                      lambda c, t=t: hT[:, c, t * TT:(t + 1) * TT], [hT_b[c][t] for c in range(NCH)], TT, nrm)

        load_w(wf[:, :, :], kview(w_in_d)[:, :, 3 * D:3 * D + H], wf_b, "wf")
        for t in range(NT):
            bk, bb = ps_aux.next()
            for kc in range(NCH):
                E("pe", lambda e, kc=kc, t=t, bk=bk: e.matmul(bk[0:H, :], lhsT=wf[:, kc, :],
                                                              rhs=hT[:, kc, t * TT:(t + 1) * TT],
                                                              start=(kc == 0), stop=(kc == NCH - 1)),
                  reads=[wf_b, hT_b[kc][t]], writes=[bb])
            E("act", lambda e, bk=bk: e.activation(out=et[:, :], in_=bk[0:H, :], func=AF.Exp, bias=negbf[:, :], scale=-1.0),
              reads=[bb, b_const], writes=[et_b])
            E("act", lambda e: e.activation(out=lt[:, :], in_=et[:, :], func=AF.Ln, bias=1.0, scale=1.0),
              reads=[et_b], writes=[lt_b])
            ci = t % 2
            init = 0.0 if t == 0 else cumt[1 - ci][:, TT - 1:TT]
            E("dve", lambda e, ci=ci, init=init: e.tensor_tensor_scan(out=cumt[ci][:, :], data0=ones16[:, :], data1=lt[:, :],
                                                                      initial=init, op0=ALU.mult, op1=ALU.subtract),
              reads=[lt_b, cum_b[1 - ci], b_const], writes=[cum_b[ci]])
            sl = slice(t * TT, (t + 1) * TT)
            E("dve", lambda e, ci=ci, sl=sl: e.tensor_copy(out=qs_full[:, 0, sl], in_=cumt[ci][:, :]),
              reads=[cum_b[ci]], writes=[qs_b])
            E("dve", lambda e, ci=ci, sl=sl: e.tensor_tensor(out=r1[:, :], in0=cumt[ci][:, :], in1=qs_full[:, 0, sl], op=ALU.subtract),
              reads=[cum_b[ci], qs_b], writes=[r1_b])
            E("dve", lambda e, sl=sl: e.tensor_copy(out=qs_full[:, 1, sl], in_=r1[:, :]), reads=[r1_b], writes=[qs_b])
            E("dve", lambda e, sl=sl: e.tensor_tensor(out=r1[:, :], in0=r1[:, :], in1=qs_full[:, 1, sl], op=ALU.subtract),
              reads=[r1_b, qs_b], writes=[r1_b])
            E("dve", lambda e, sl=sl: e.tensor_copy(out=qs_full[:, 2, sl], in_=r1[:, :]), reads=[r1_b], writes=[qs_b])

        def proj_group(g):
            i = g % 2
            for part in range(3):
                load_w(wq[i][:, :, part * 128:(part + 1) * 128],
                       kview(w_in_d)[:, :, part * D + g * 128: part * D + (g + 1) * 128], wq_b[i], "wq%d" % i)
            sq, sq_b, lnv, lnv_b, rstd, rstd_b = nrm
            si = 0
            for part, dst, dst_b in ((0, qa, qa_b), (1, ka, ka_b)):
                for t in range(NT):
                    bk, bb = ps_aux.next()
                    for kc in range(NCH):
                        E("pe", lambda e, kc=kc, t=t, bk=bk, part=part: e.matmul(
                            bk[:, :], lhsT=wq[i][:, kc, part * 128:(part + 1) * 128], rhs=hT[:, kc, t * TT:(t + 1) * TT],
                            start=(kc == 0), stop=(kc == NCH - 1)),
                          reads=[wq_b[i], hT_b[kc][t]], writes=[bb])
                    s_ = si % 2
                    si += 1
                    yield
                    yield
                    yield
                    E("act", lambda e, bk=bk, s_=s_: e.activation(out=sq[s_][:, :], in_=bk[:, :], func=AF.Square),
                      reads=[bb], writes=[sq_b[s_]])
                    yield
                    yield
                    sk, sbb = ps_aux.next()
                    E("pe", lambda e, sk=sk, s_=s_: e.matmul(sk[:, :], lhsT=blk_b[:, :], rhs=sq[s_][:, :], start=True, stop=True),
                      reads=[sq_b[s_], b_const], writes=[sbb])
                    yield
                    yield
                    E("act", lambda e, sk=sk: e.activation(out=lnv[:, :], in_=sk[:, :], func=AF.Ln, bias=EPS, scale=1.0 / DH),
                      reads=[sbb], writes=[lnv_b])
                    E("act", lambda e: e.activation(out=rstd[:, :], in_=lnv[:, :], func=AF.Exp, scale=-0.5),
                      reads=[lnv_b], writes=[rstd_b])
                    for j in range(2):
                        rows = slice(64 * j, 64 * j + 64)
                        gc = gqs[rows, g:g + 1] if part == 0 else cols1[rows, g, 2:3]
                        E("dve", lambda e, j=j, rows=rows, gc=gc, bk=bk, t=t, dst=dst: e.scalar_tensor_tensor(
                            out=dst[i][0:64, j, t * TT:(t + 1) * TT], in0=bk[rows, :], scalar=gc, in1=rstd[rows, :],
                            op0=ALU.mult, op1=ALU.mult),
                          reads=[bb, rstd_b, b_const], writes=[dst_b[i]])
                    yield
            for j in range(2):
                h = 2 * g + j
                E("sp", lambda e, j=j, h=h: e.dma_start(out=qa[i][64:67, j, :], in_=qs_full[h:h + 1, :, :]),
                  reads=[qs_b], writes=[qaaug_b[i]], dma="aug%d" % i)
                E("sp", lambda e, j=j, h=h: e.dma_start(out=ka[i][67:70, j, :], in_=qs_full[h:h + 1, :, :]),
                  reads=[qs_b], writes=[kaaug_b[i]], dma="aug%d" % i)
            for t in range(NT):
                bk, bb = ps_aux.next()
                for jj in range(4):
                    tt = 4 * t + jj
                    for kc in range(NCH):
                        E("pe", lambda e, kc=kc, tt=tt, jj=jj, bk=bk: e.matmul(
                            bk[:, jj * 128:(jj + 1) * 128], lhsT=hT[:, kc, tt * 128:(tt + 1) * 128], rhs=wq[i][:, kc, 256:384],
                            start=(kc == 0), stop=(kc == NCH - 1)),
                          reads=[wq_b[i], hT_b[kc][t]], writes=[bb])
                    if jj % 2 == 1:
                        yield
                bv = bk[:, :].rearrange("p (a c) -> p a c", a=4)
                E("dve", lambda e, t=t, bv=bv: e.tensor_copy(out=va[i][:, 4 * t:4 * t + 4, 0:64], in_=bv[:, :, 0:64]),
                  reads=[bb], writes=[va_b[i]])
                E("dve", lambda e, t=t, bv=bv: e.tensor_copy(out=va[i][:, 4 * t:4 * t + 4, 128:192], in_=bv[:, :, 64:128]),
                  reads=[bb], writes=[va_b[i]])
                yield

        pti = [0]
        reci = [0]

        LA = 2

        def attn_group(g, filler=None):
            i = g % 2
            jobs = []
            for j in range(2):
                for I in range(NT):
                    nk = 4 * I + 4
                    for kt in range(nk):
                        jobs.append((j, I, kt, nk))
            st = {}
            obs = {}

            def s_stage(n):
                j, I, kt, nk = jobs[n]
                r = kt - 4 * I
                c0 = max(0, r) * 128
                sk, sbb = ps_s.next()
                E("pe", lambda e, kt=kt, c0=c0, sk=sk, I=I, j=j: e.matmul(
                    sk[:, c0:TT], lhsT=ka[i][0:70, j, kt * 128:(kt + 1) * 128],
                    rhs=qa[i][0:70, j, I * TT + c0:(I + 1) * TT], start=True, stop=True),
                  reads=[ka_b[i], kaaug_b[i], qa_b[i], qaaug_b[i]], writes=[sbb])
                if r >= 0:
                    E("dve", lambda e, c0=c0, sk=sk: e.tensor_tensor(out=sk[:, c0:c0 + 128], in0=sk[:, c0:c0 + 128],
                                                                     in1=maskneg[:, :], op=ALU.add),
                      reads=[sbb, b_const], writes=[sbb])
                pi = pti[0] % len(PT)
                pti[0] += 1
                E("act", lambda e, c0=c0, sk=sk, pi=pi: e.activation(out=PT[pi][:, c0:TT], in_=sk[:, c0:TT], func=AF.Exp),
                  reads=[sbb], writes=[PT_b[pi]])
                st[n] = (pi, c0)

            def pv_stage(n):
                j, I, kt, nk = jobs[n]
                pi, c0 = st.pop(n)
                if kt == 0:
                    obs[(j, I)] = ps_o.next()
                ob, obb = obs[(j, I)]
                E("pe", lambda e, kt=kt, c0=c0, pi=pi, ob=ob, j=j, nk=nk: e.matmul(
                    ob[:, c0:TT], lhsT=va[i][:, kt, 64 * j:64 * j + 128], rhs=PT[pi][:, c0:TT],
                    start=(kt == 0), stop=(kt == nk - 1)),
                  reads=[va_b[i], PT_b[pi]], writes=[obb])
                if kt == nk - 1:
                    num = slice(64 * j, 64 * j + 64)
                    den = slice(64 * (1 - j), 64 * (1 - j) + 64)
                    ri = reci[0] % 2
                    reci[0] += 1
                    E("act", lambda e, ob=ob, den=den, ri=ri: e.activation(out=rec[ri][den, :], in_=ob[den, :], func=AF.Ln),
                      reads=[obb], writes=[rec_b[ri]])
                    E("act", lambda e, den=den, ri=ri: e.activation(out=rec[ri][den, :], in_=rec[ri][den, :], func=AF.Exp, scale=-1.0),
                      reads=[rec_b[ri]], writes=[rec_b[ri]])
                    E("dve", lambda e, ob=ob, num=num, den=den, ri=ri, I=I: e.tensor_tensor(
                        out=oT[num, g, I * TT:(I + 1) * TT], in0=ob[num, :], in1=rec[ri][den, :], op=ALU.mult),
                      reads=[obb, rec_b[ri]], writes=[oT_b[g][I]])

            N = len(jobs)
            for n in range(min(LA, N)):
                s_stage(n)
            for n in range(N):
                if n + LA < N:
                    s_stage(n + LA)
                pv_stage(n)
                if filler is not None:
                    next(filler, None)
            if filler is not None:
                for _ in filler:
                    pass

        NG = H // 2
        for _ in proj_group(0):
            pass
        for g in range(NG):
            attn_group(g, proj_group(g + 1) if g + 1 < NG else None)

        if n_phases <= 1 and False:
            pass

        xT = sb("xT%d" % b, [128, NCH, S], F32, BASE + 64 * KB)
        xT_b = [[P.fresh() for _ in range(NT)] for _ in range(NCH)]
        o2 = BASE + 128 * KB
        xin2 = [sb("xin2_%d_%d" % (i, b), [128, D], F32, o2 + i * 4 * KB) for i in range(4)]; o2 += 16 * KB
        wo2 = [sb("wo2_%d_%d" % (i, b), [128, NCH, 512], BF16, o2 + i * 8 * KB) for i in range(2)]; o2 += 16 * KB
        assert o2 <= BASE + ARENA
        xin2_b = P.fresh(4); wo2_b = P.fresh(2)
        for tt in range(NTT):
            t = tt // 4
            load_x_T(b, tt, xin2, xin2_b,
                     lambda hh, tt=tt: xT[:, 4 * hh:4 * hh + 4, tt * 128:(tt + 1) * 128],
                     [xT_b[c][t] for c in range(NCH)])
        for half in range(2):
            load_w(wo2[half][:, :, :], kview(w_o_d)[:, :, half * 512:(half + 1) * 512], wo2_b[half], "wo%d" % half)
            for t in range(NT):
                for cc in range(4):
                    c = 4 * half + cc
                    bk, bb = ps_mm.next()
                    for kc in range(NCH):
                        E("pe", lambda e, kc=kc, t=t, cc=cc, bk=bk, half=half: e.matmul(
                            bk[:, :], lhsT=wo2[half][:, kc, cc * 128:(cc + 1) * 128], rhs=oT[:, kc, t * TT:(t + 1) * TT],
                            start=(kc == 0), stop=(kc == NCH - 1)),
                          reads=[wo2_b[half], oT_b[kc][t]], writes=[bb])
                    E("dve", lambda e, c=c, t=t, bk=bk: e.tensor_tensor(out=xT[:, c, t * TT:(t + 1) * TT], in0=bk[:, :],
                                                                        in1=xT[:, c, t * TT:(t + 1) * TT], op=ALU.add),
                      reads=[bb, xT_b[c][t]], writes=[xT_b[c][t]])

        def ffn_phase(li):
            o = BASE
            hT2 = sb("hT2_%d_%d" % (li, b), [128, NCH, S], BF16, o); o += 32 * KB
            actb = sb("actb_%d_%d" % (li, b), [128, 4, S], BF16, o); o += 16 * KB
            nrm2, sz = nrm_alloc(o); o += sz
            sg = [sb("sg%d_%d_%d" % (i, li, b), [128, TT], F32, o + i * 2 * KB) for i in range(2)]; o += 4 * KB
            o = BASE + 128 * KB
            wgrp = []
            for i in range(2):
                wg_ = sb("wg%d_%d_%d" % (i, li, b), [128, NCH, 512], BF16, o); o += 8 * KB
                wv_ = sb("wv%d_%d_%d" % (i, li, b), [128, NCH, 512], BF16, o); o += 8 * KB
                wd_ = sb("wd%d_%d_%d" % (i, li, b), [128, 4, D], BF16, o); o += 8 * KB
                wgrp.append((wg_, wv_, wd_))
            ug = [sb("ug%d_%d_%d" % (i, li, b), [128, TT + 2], F32, o + i * 2080) for i in range(2)]; o += 2 * 2080
            uv = [sb("uv%d_%d_%d" % (i, li, b), [128, TT + 2], F32, o + i * 2080) for i in range(2)]; o += 2 * 2080
            ag = [sb("ag%d_%d_%d" % (i, li, b), [128, TT], F32, o + i * 2 * KB) for i in range(2)]; o += 4 * KB
            av = [sb("av%d_%d_%d" % (i, li, b), [128, TT], F32, o + i * 2 * KB) for i in range(2)]; o += 4 * KB
            assert o <= BASE + ARENA, ("ffn", o)
            hT2_b = [[P.fresh() for _ in range(NT)] for _ in range(NCH)]
            act_b = [[P.fresh() for _ in range(NT)] for _ in range(4)]
            wgrp_b = P.fresh(2)
            ug_b = P.fresh(2); uv_b = P.fresh(2); ag_b = P.fresh(2); av_b = P.fresh(2); sg_b = P.fresh(2)
            uh_b = {id(ug): P.fresh(2), id(uv): P.fresh(2)}
            for t in range(NT):
                norm_tile(lambda c, t=t: xT[:, c, t * TT:(t + 1) * TT], [xT_b[c][t] for c in range(NCH)],
                          lambda c: col1(c, 9 + li),
                          lambda c, t=t: hT2[:, c, t * TT:(t + 1) * TT], [hT2_b[c][t] for c in range(NCH)], TT, nrm2)
            groups = [(f0, min(4, NFC - f0)) for f0 in range(0, NFC, 4)]
            r0 = 4 * li
            ui = [0]

            def stage_a(wi, wg_, wv_, fc, fch, t):
                k = ui[0] % 2
                ui[0] += 1
                pk = 1 - k
                specs = ((wg_, ug, ug_b, ag, ag_b, fch), (wv_, uv, uv_b, av, av_b, NFC + fch))
                for (wsrc, ub, ub_b, acc, acc_b, chn) in specs:
                    bk, bb = ps_up.next()
                    for kc in range(NCH):
                        E("pe", lambda e, kc=kc, t=t, bk=bk, wsrc=wsrc, fc=fc: e.matmul(
                            bk[:, :], lhsT=wsrc[:, kc, fc * 128:(fc + 1) * 128], rhs=hT2[:, kc, t * TT:(t + 1) * TT],
                            start=(kc == 0), stop=(kc == NCH - 1)),
                          reads=[wgrp_b[wi], hT2_b[kc][t]], writes=[bb])
                    hb_b = uh_b[id(ub)]
                    if t == 0:
                        E("act", lambda e, ub=ub, k=k: e.activation(out=ub[k][:, 0:2], in_=zeros_f[:, 0:2], func=AF.Copy),
                          reads=[b_const], writes=[hb_b[k]])
                    else:
                        E("act", lambda e, ub=ub, k=k, pk=pk: e.activation(out=ub[k][:, 0:2], in_=ub[pk][:, TT:TT + 2], func=AF.Copy),
                          reads=[ub_b[pk]], writes=[hb_b[k]])
                    E("act", lambda e, ub=ub, k=k, bk=bk: e.activation(out=ub[k][:, 2:TT + 2], in_=bk[:, :], func=AF.Copy),
                      reads=[bb], writes=[ub_b[k]])
                    E("act", lambda e, k=k, acc=acc, chn=chn, bk=bk: e.activation(
                        out=acc[k][:, :], in_=bk[:, :], func=AF.Identity,
                        bias=cols2[:, chn, r0 + 3:r0 + 4], scale=cols2[:, chn, r0 + 2:r0 + 3]),
                      reads=[bb, b_const], writes=[acc_b[k]])
                for tap in (1, 0):
                    for (wsrc, ub, ub_b, acc, acc_b, chn) in specs:
                        E("dve", lambda e, ub=ub, k=k, acc=acc, chn=chn, tap=tap: e.scalar_tensor_tensor(
                            out=acc[k][:, :], in0=ub[k][:, tap:tap + TT], scalar=cols2[:, chn, r0 + tap:r0 + tap + 1],
                            in1=acc[k][:, :], op0=ALU.mult, op1=ALU.add),
                          reads=[ub_b[k], uh_b[id(ub)][k], acc_b[k], b_const], writes=[acc_b[k]])
                return k

            def stage_b(k, fc, t):
                E("act", lambda e, k=k: e.activation(out=sg[k][:, :], in_=ag[k][:, :], func=AF.Silu),
                  reads=[ag_b[k]], writes=[sg_b[k]])
                E("pool", lambda e, k=k, fc=fc, t=t: e.tensor_tensor(out=actb[:, fc, t * TT:(t + 1) * TT], in0=sg[k][:, :],
                                                                     in1=av[k][:, :], op=ALU.mult),
                  reads=[sg_b[k], av_b[k]], writes=[act_b[fc][t]])

            def load_group(gi):
                f0, n = groups[gi]
                wi = gi % 2
                wg_, wv_, wd_ = wgrp[wi]
                load_w(wg_[:, :, 0:n * 128], kview(wup_d[li])[:, :, f0 * 128:(f0 + n) * 128], wgrp_b[wi], "wg%d" % wi)
                load_w(wv_[:, :, 0:n * 128], kview(wup_d[li])[:, :, FF + f0 * 128:FF + (f0 + n) * 128], wgrp_b[wi], "wg%d" % wi)
                load_w(wd_[:, 0:n, :], wdn_d[li][f0 * 128:(f0 + n) * 128, :].rearrange("(fc p) d -> p fc d", p=128),
                       wgrp_b[wi], "wg%d" % wi)

            load_group(0)
            for gi, (f0, n) in enumerate(groups):
                wi = gi % 2
                wg_, wv_, wd_ = wgrp[wi]
                pend = None
                for fc in range(n):
                    fch = f0 + fc
                    for t in range(NT):
                        k = stage_a(wi, wg_, wv_, fc, fch, t)
                        if pend is not None:
                            stage_b(*pend)
                        pend = (k, fc, t)
                    if fc == 0 and gi + 1 < len(groups):
                        load_group(gi + 1)
                stage_b(*pend)
                for t in range(NT):
                    for c in range(NCH):
                        bk, bb = ps_o.next()
                        for fc in range(n):
                            E("pe", lambda e, fc=fc, t=t, c=c, bk=bk, wd_=wd_: e.matmul(
                                bk[:, :], lhsT=wd_[:, fc, c * 128:(c + 1) * 128], rhs=actb[:, fc, t * TT:(t + 1) * TT],
                                start=(fc == 0), stop=(fc == n - 1)),
                              reads=[wgrp_b[wi], act_b[fc][t]], writes=[bb])
                        E("dve", lambda e, c=c, t=t, bk=bk: e.tensor_tensor(out=xT[:, c, t * TT:(t + 1) * TT], in0=bk[:, :],
                                                                            in1=xT[:, c, t * TT:(t + 1) * TT], op=ALU.add),
                          reads=[bb, xT_b[c][t]], writes=[xT_b[c][t]])

        def conv_phase():
            o = BASE
            hT3 = sb("hT3_%d" % b, [128, NCH, S], BF16, o); o += 32 * KB
            glu = sb("glu_%d" % b, [128, NCH, S], BF16, o); o += 32 * KB
            o = BASE + 128 * KB
            cv = sb("cv_%d" % b, [128, NCH, S], BF16, o); o += 32 * KB
            diag2 = [sb("diag%d_%d" % (i, b), [128, CW, 128], BF16, o + i * CW * 256) for i in range(2)]; o += 2 * CW * 256
            nrm3, sz = nrm_alloc(o); o += sz
            sgm = sb("sgm_%d" % b, [128, TT], F32, o); o += 2 * KB
            tmpn = sb("tmpn_%d" % b, [128, TT], F32, o); o += 2 * KB
            wpa = sb("wpa_%d" % b, [128, NCH, 512], BF16, o); o += 8 * KB
            wpg = sb("wpg_%d" % b, [128, NCH, 512], BF16, BASE + 128 * KB)
            assert o <= BASE + ARENA, ("conv", o)
            hT3_b = [[P.fresh() for _ in range(NT)] for _ in range(NCH)]
            glu_b = [[P.fresh() for _ in range(NT)] for _ in range(NCH)]
            diag_bb = [[P.fresh() for _ in range(CW)] for _ in range(2)]; sgm_b = P.fresh(); tmpn_b = P.fresh(); wpa_b = P.fresh(); wpg_b = P.fresh()
            for t in range(NT):
                norm_tile(lambda c, t=t: xT[:, c, t * TT:(t + 1) * TT], [xT_b[c][t] for c in range(NCH)],
                          lambda c: col1(c, 3),
                          lambda c, t=t: hT3[:, c, t * TT:(t + 1) * TT], [hT3_b[c][t] for c in range(NCH)], TT, nrm3)
            for grp in range(2):
                load_w(wpa[:, :, :], kview(pw1_d)[:, :, grp * 512:(grp + 1) * 512], wpa_b, "wpa")
                load_w(wpg[:, :, :], kview(pw1_d)[:, :, D + grp * 512:D + (grp + 1) * 512], wpg_b, "wpg")
                for cc in range(4):
                    c = 4 * grp + cc
                    for t in range(NT):
                        ab, abb = ps_mm.next()
                        gb, gbb = ps_mm.next()
                        for (wsrc, wb, bk, bb) in ((wpa, wpa_b, ab, abb), (wpg, wpg_b, gb, gbb)):
                            for kc in range(NCH):
                                E("pe", lambda e, kc=kc, t=t, bk=bk, wsrc=wsrc, cc=cc: e.matmul(
                                    bk[:, :], lhsT=wsrc[:, kc, cc * 128:(cc + 1) * 128], rhs=hT3[:, kc, t * TT:(t + 1) * TT],
                                    start=(kc == 0), stop=(kc == NCH - 1)),
                                  reads=[wb, hT3_b[kc][t]], writes=[bb])
                        E("act", lambda e, gb=gb, c=c: e.activation(out=sgm[:, :], in_=gb[:, :], func=AF.Sigmoid,
                                                                    bias=col1(c, 5), scale=1.0),
                          reads=[gbb, b_const], writes=[sgm_b])
                        E("dve", lambda e, ab=ab, c=c, t=t: e.scalar_tensor_tensor(
                            out=glu[:, c, t * TT:(t + 1) * TT], in0=ab[:, :], scalar=col1(c, 4), in1=sgm[:, :],
                            op0=ALU.add, op1=ALU.mult),
                          reads=[abb, sgm_b, b_const], writes=[glu_b[c][t]])
            cv_b = [[P.fresh() for _ in range(NT)] for _ in range(NCH)]
            for c in range(NCH):
                diag = diag2[c % 2]
                diag_b = diag_bb[c % 2]
                for k in range(CW):
                    if k % 2 == 0:
                        E("act", lambda e, k=k, c=c, diag=diag: e.activation(out=diag[:, k, :], in_=ident_b[:, :], func=AF.Copy,
                                                                              scale=col1(c, 11 + k)),
                          reads=[b_const], writes=[diag_b[k]])
                    else:
                        E("dve", lambda e, k=k, c=c, diag=diag: e.tensor_scalar(out=diag[:, k, :], in0=ident_b[:, :],
                                                                                scalar1=col1(c, 11 + k), scalar2=None, op0=ALU.mult),
                          reads=[b_const], writes=[diag_b[k]])
                for t in range(NT):
                    bk, bb = ps_mm.next()
                    first = True
                    for k in range(CW - 1, -1, -1):
                        s = CW - 1 - k
                        lo = max(0, s - t * TT)
                        srcs = [glu_b[c][t]] + ([glu_b[c][t - 1]] if (t > 0 and s > 0) else [])
                        E("pe", lambda e, k=k, s=s, lo=lo, t=t, c=c, bk=bk, first=first, diag=diag: e.matmul(
                            bk[:, lo:TT], lhsT=diag[:, k, :], rhs=glu[:, c, t * TT + lo - s:(t + 1) * TT - s],
                            start=first, stop=(k == 0)),
                          reads=[diag_b[k]] + srcs, writes=[bb])
                        first = False
                    E("act", lambda e, c=c, t=t, bk=bk: e.activation(out=cv[:, c, t * TT:(t + 1) * TT], in_=bk[:, :],
                                                                     func=AF.Identity, bias=col1(c, 6), scale=1.0),
                      reads=[bb, b_const], writes=[cv_b[c][t]])
            cT = hT3
            cT_b = hT3_b
            sq, sq_b, lnv, lnv_b, rstd, rstd_b = nrm3
            wpb = sb("wpb_%d" % b, [128, NCH, 512], BF16, BASE + 160 * KB)
            wpb_b = P.fresh()
            wp2 = ((wpa, wpa_b, "wpa"), (wpb, wpb_b, "wpb"))
            for half in range(2):
                load_w(wp2[half][0][:, :, :], kview(pw2_d)[:, :, half * 512:(half + 1) * 512], wp2[half][1], wp2[half][2])

            def norm2(t):
                ssb, ssbb = ps_aux.next()
                for c in range(NCH):
                    i = c % 2
                    E("act", lambda e, c=c, i=i, t=t: e.activation(out=sq[i][:, :], in_=cv[:, c, t * TT:(t + 1) * TT], func=AF.Square),
                      reads=[cv_b[c][t]], writes=[sq_b[i]])
                    E("pe", lambda e, c=c, i=i, ssb=ssb: e.matmul(ssb[:, :], lhsT=ones_b[:, :], rhs=sq[i][:, :],
                                                                  start=(c == 0), stop=(c == NCH - 1)),
                      reads=[sq_b[i], b_const], writes=[ssbb])
                E("act", lambda e, ssb=ssb: e.activation(out=lnv[:, :], in_=ssb[:, :], func=AF.Ln, bias=EPS, scale=1.0 / D),
                  reads=[ssbb], writes=[lnv_b])
                E("act", lambda e: e.activation(out=rstd[:, :], in_=lnv[:, :], func=AF.Exp, scale=-0.5),
                  reads=[lnv_b], writes=[rstd_b])
                for c in range(NCH):
                    E("dve", lambda e, c=c, t=t: e.scalar_tensor_tensor(out=tmpn[:, :], in0=cv[:, c, t * TT:(t + 1) * TT],
                                                                        scalar=col1(c, 7), in1=rstd[:, :], op0=ALU.mult, op1=ALU.mult),
                      reads=[cv_b[c][t], rstd_b, b_const], writes=[tmpn_b])
                    E("act", lambda e, c=c, t=t: e.activation(out=cT[:, c, t * TT:(t + 1) * TT], in_=tmpn[:, :], func=AF.Silu),
                      reads=[tmpn_b], writes=[cT_b[c][t]])

            def pw2(t):
                for half in range(2):
                    wsrc, wb, _ = wp2[half]
                    for cc in range(4):
                        c = 4 * half + cc
                        bk, bb = ps_mm.next()
                        for kc in range(NCH):
                            E("pe", lambda e, kc=kc, t=t, cc=cc, bk=bk, wsrc=wsrc: e.matmul(
                                bk[:, :], lhsT=wsrc[:, kc, cc * 128:(cc + 1) * 128], rhs=cT[:, kc, t * TT:(t + 1) * TT],
                                start=(kc == 0), stop=(kc == NCH - 1)),
                              reads=[wb, cT_b[kc][t]], writes=[bb])
                        E("dve", lambda e, c=c, t=t, bk=bk: e.scalar_tensor_tensor(
                            out=xT[:, c, t * TT:(t + 1) * TT], in0=bk[:, :], scalar=col1(c, 8), in1=xT[:, c, t * TT:(t + 1) * TT],
                            op0=ALU.add, op1=ALU.add),
                          reads=[bb, xT_b[c][t], b_const], writes=[xT_b[c][t]])

            norm2(0)
            for t in range(NT):
                if t + 1 < NT:
                    norm2(t + 1)
                pw2(t)

        if n_phases >= 2:
            ffn_phase(0)
        if n_phases >= 3:
            conv_phase()
        if n_phases >= 4:
            ffn_phase(1)

        o = BASE
        otile = [sb("otile%d_%d" % (i, b), [128, D], F32, o + i * 4 * KB) for i in range(4)]
        otile_b = P.fresh(4)
        for tt in range(NTT):
            t = tt // 4
            i = tt % 4
            for hh in range(2):
                bk, bb = ps_aux.next()
                for cc in range(4):
                    c = 4 * hh + cc
                    E("pe", lambda e, c=c, cc=cc, bk=bk, tt=tt: e.transpose(out=bk[:, cc * 128:(cc + 1) * 128],
                                                                            in_=xT[:, c, tt * 128:(tt + 1) * 128],
                                                                            identity=ident_f[:, :]),
                      reads=[xT_b[c][t], b_const], writes=[bb])
                if hh == 0:
                    E("act", lambda e, hh=hh, bk=bk, i=i: e.activation(out=otile[i][:, hh * 512:(hh + 1) * 512], in_=bk[:, :], func=AF.Copy),
                      reads=[bb], writes=[otile_b[i]])
                else:
                    E("dve", lambda e, hh=hh, bk=bk, i=i: e.tensor_copy(out=otile[i][:, hh * 512:(hh + 1) * 512], in_=bk[:, :]),
                      reads=[bb], writes=[otile_b[i]])
            E("sp", lambda e, i=i, tt=tt: e.dma_start(out=y_d[b, tt * 128:(tt + 1) * 128, :], in_=otile[i][:, :]),
              reads=[otile_b[i]], dma="out%d" % i)

    for b_ in range(NSEQ):
        seq_body(b_)

    dma_names = sorted(P.dma_count.keys())
    final_dma = dict(P.dma_count)
    nsem_eng = {e: (P.count[e] + LIMIT - 1) // LIMIT for e in ENGS}

    import contextlib
    with contextlib.ExitStack() as st:
        sems = {}
        for e in ENGS:
            for ep in range(max(1, nsem_eng[e])):
                sems[("e", e, ep)] = st.enter_context(nc.semaphore("s_%s_%d" % (e, ep)))
        for n_ in dma_names:
            sems[("d", n_)] = st.enter_context(nc.semaphore("d_" + n_))
        block = st.enter_context(nc.Block())

        def resolve(k, v):
            if k[0] == "e":
                ep = (v - 1) // LIMIT
                return sems[("e", k[1], ep)], v - ep * LIMIT
            return sems[("d", k[1])], v

        def replay(name, eng, final=False):
            for waits, fn, tok in P.streams[name]:
                for k, v in waits:
                    s_, val = resolve(k, v)
                    eng.wait_ge(s_, val)
                inst = fn(eng)
                if tok[0][0] == "e":
                    s_, _ = resolve(tok[0], tok[1])
                    inst.then_inc(s_, 1)
                else:
                    inst.then_inc(sems[("d", tok[0][1])], 16)
            if final:
                for n_ in dma_names:
                    if n_.startswith("out"):
                        eng.wait_ge(sems[("d", n_)], final_dma[n_])

        @block.sync
        def _(e):
            replay("sp", e, final=True)

        @block.tensor
        def _(e):
            replay("pe", e)

        @block.scalar
        def _(e):
            replay("act", e)

        @block.vector
        def _(e):
            replay("dve", e)

        @block.gpsimd
        def _(e):
            replay("pool", e)

    return nc, {e: P.count[e] for e in ENGS}


def pack_params(inp):
    pvec = np.concatenate([
        inp["fox_norm_g"].reshape(1, D),
        inp["fox_q_g"].reshape(1, D),
        inp["fox_k_g"].reshape(1, D),
        inp["conv_norm_g"].reshape(1, D),
        inp["conv_b_pw1"].reshape(2, D),
        inp["conv_b_dw"].reshape(1, D),
        inp["conv_ln_g"].reshape(1, D),
        inp["conv_b_pw2"].reshape(1, D),
        inp["ffn_norm_g"].reshape(2, D),
        inp["conv_w_dw"].reshape(CW, D),
    ], axis=0).astype(np.float32)
    p2 = np.concatenate([
        inp["ffn_w_dw"][0].reshape(3, 2 * FF), inp["ffn_b_dw"][0].reshape(1, 2 * FF),
        inp["ffn_w_dw"][1].reshape(3, 2 * FF), inp["ffn_b_dw"][1].reshape(1, 2 * FF),
    ], axis=0).astype(np.float32)
    return np.ascontiguousarray(pvec), np.ascontiguousarray(p2)


_CACHE = {}


def run(inputs, n_phases=99, ncores=NCORES, max_seq_per_launch=4):
    inp = {k: np.asarray(v) for k, v in inputs.items()}
    x = inp["x"]
    B = x.shape[0]
    NSEQ_TOT = B // ncores
    NSEQ = min(NSEQ_TOT, max_seq_per_launch)
    nlaunch = NSEQ_TOT // NSEQ
    key = (NSEQ, n_phases)
    if key not in _CACHE:
        _CACHE[key] = build_program(NSEQ, n_phases)
    nc, counts = _CACHE[key]
    pvec, p2 = pack_params(inp)
    shared = {
        "pvec": pvec, "p2": p2,
        "b_f": np.ascontiguousarray(inp["fox_b_f"].reshape(H, 1).astype(np.float32)),
        "fox_w_in": np.ascontiguousarray(inp["fox_w_in"][0]),
        "fox_w_o": np.ascontiguousarray(inp["fox_w_o"][0]),
        "conv_w_pw1": np.ascontiguousarray(inp["conv_w_pw1"][0]),
        "conv_w_pw2": np.ascontiguousarray(inp["conv_w_pw2"][0]),
        "ffn_w_up0": np.ascontiguousarray(inp["ffn_w_up"][0]),
        "ffn_w_up1": np.ascontiguousarray(inp["ffn_w_up"][1]),
        "ffn_w_down0": np.ascontiguousarray(inp["ffn_w_down"][0]),
        "ffn_w_down1": np.ascontiguousarray(inp["ffn_w_down"][1]),
    }
    out = np.empty((B, S, D), np.float32)
    for l in range(nlaunch):
        in_maps = []
        for c in range(ncores):
            m = dict(shared)
            s0 = c * NSEQ_TOT + l * NSEQ
            m["x"] = np.ascontiguousarray(x[s0:s0 + NSEQ])
            in_maps.append(m)
        res = run_bass_kernel_spmd(nc, in_maps, core_ids=list(range(ncores)))
        for c in range(ncores):
            s0 = c * NSEQ_TOT + l * NSEQ
            out[s0:s0 + NSEQ] = res.results[c]["y"]
    return out


def kernel(**inputs):
    return run(inputs)
```

```python
import numpy as np
import concourse.bass as bass
import concourse.mybir as mybir
from concourse.bass_utils import run_bass_kernel_spmd

F32 = mybir.dt.float32
BF16 = mybir.dt.bfloat16
AF = mybir.ActivationFunctionType
ALU = mybir.AluOpType

D = 1024
S = 2048
H = 16
DH = 64
FF = 2816
NFC = FF // 128
CW = 31
EPS = 1e-6
NCH = D // 128
TT = 512
NT = S // TT
NTT = S // 128
NCORES = 8
LIMIT = 30000
ENGS = ("pe", "act", "dve", "pool", "sp")
KB = 1024
NEG = -30000.0


class Buf:
    __slots__ = ("w", "r")

    def __init__(self):
        self.w = None
        self.r = {}


class Plan:
    def __init__(self):
        self.streams = {e: [] for e in ENGS}
        self.count = {e: 0 for e in ENGS}
        self.seen = {e: {} for e in ENGS}
        self.dma_count = {}

    def emit(self, eng, fn, reads=(), writes=(), dma=None):
        need = {}
        own = ("e", eng)
        for b in reads:
            if b.w is not None:
                k, v = b.w
                if need.get(k, 0) < v:
                    need[k] = v
        for b in writes:
            if b.w is not None:
                k, v = b.w
                if need.get(k, 0) < v:
                    need[k] = v
            for k, v in b.r.items():
                if k == own:
                    continue
                if need.get(k, 0) < v:
                    need[k] = v
        waits = []
        seen = self.seen[eng]
        for k, v in need.items():
            if k == own and eng == "pe":
                continue
            if seen.get(k, 0) >= v:
                continue
            seen[k] = v
            waits.append((k, v))
        if dma is None:
            self.count[eng] += 1
            tok = (own, self.count[eng])
        else:
            self.dma_count[dma] = self.dma_count.get(dma, 0) + 16
            tok = (("d", dma), self.dma_count[dma])
        self.streams[eng].append((waits, fn, tok))
        for b in reads:
            b.r[tok[0]] = tok[1]
        for b in writes:
            b.w = tok
            b.r = {}
        return tok

    def all_tokens(self):
        d = {("e", e): c for e, c in self.count.items() if c > 0}
        for k, v in self.dma_count.items():
            d[("d", k)] = v
        return d

    def fresh(self, n=None):
        def mk():
            b = Buf()
            b.r = self.all_tokens()
            return b
        if n is None:
            return mk()
        return [mk() for _ in range(n)]


def build_program(NSEQ, n_phases=99):
    nc = bass.Bass("TRN2", target_bir_lowering=False)
    P = Plan()

    def din(name, shape):
        return nc.dram_tensor(name, list(shape), F32, kind="ExternalInput").ap()

    x_d = din("x", [NSEQ, S, D])
    y_d = nc.dram_tensor("y", [NSEQ, S, D], F32, kind="ExternalOutput").ap()
    pvec_d = din("pvec", [42, D])
    p2_d = din("p2", [8, 2 * FF])
    bf_d = din("b_f", [H, 1])
    w_in_d = din("fox_w_in", [D, 3 * D + H])
    w_o_d = din("fox_w_o", [D, D])
    pw1_d = din("conv_w_pw1", [D, 2 * D])
    pw2_d = din("conv_w_pw2", [D, D])
    wup_d = [din("ffn_w_up%d" % i, [D, 2 * FF]) for i in range(2)]
    wdn_d = [din("ffn_w_down%d" % i, [FF, D]) for i in range(2)]

    def kview(w):
        return w.rearrange("(kc p) n -> p kc n", p=128)

    def sb(name, shape, dt, off):
        return nc.alloc_sbuf_tensor_at(name, list(shape), dt, offset=off)

    o = 16640
    ident_f = sb("ident_f", [128, 128], F32, o); o += 512
    ident_b = sb("ident_b", [128, 128], BF16, o); o += 256
    ones_b = sb("ones_b", [128, 128], BF16, o); o += 256
    blk_b = sb("blk_b", [128, 128], BF16, o); o += 256
    maskneg = sb("maskneg", [128, 128], F32, o); o += 512
    cols1 = sb("cols1", [128, NCH, 42], F32, o); o += NCH * 42 * 4
    cols2 = sb("cols2", [128, 2 * NFC, 8], F32, o); o += 2 * NFC * 8 * 4
    negbf = sb("negbf", [H, 1], F32, o); o += 32
    gqs = sb("gqs", [128, NCH], F32, o); o += 32
    zeros_f = sb("zeros_f", [128, 128], F32, o); o += 512
    ones16 = sb("ones16", [H, TT], F32, o); o += TT * 4
    maskneg_b = sb("maskneg_b", [128, 128], BF16, o); o += 256
    CONST_END = (o + 63) // 64 * 64
    BASE = CONST_END
    ARENA = 16640 + 207 * KB - BASE

    b_const = Buf()

    banks = [nc.alloc_psum_tensor("bank%d" % i, [128, 512], F32) for i in range(8)]
    bank_buf = [Buf() for _ in range(8)]

    class Pool_:
        def __init__(self, ids):
            self.ids = ids
            self.i = 0

        def next(self):
            b = self.ids[self.i % len(self.ids)]
            self.i += 1
            return banks[b], bank_buf[b]

    ps_mm = Pool_([0, 1, 2, 3])
    ps_s = Pool_([0, 1, 2])
    ps_aux = Pool_([3, 4, 5])
    ps_o = Pool_([6, 7])
    ps_up = Pool_([0, 1, 2, 3, 4, 5])

    E = P.emit

    stage1 = sb("stage1", [42, D], F32, BASE)
    stage2 = sb("stage2", [8, 2 * FF], F32, BASE + D * 4)
    b_stage1 = Buf()
    b_stage2 = Buf()
    E("sp", lambda e: e.dma_start(out=stage1[:, :], in_=pvec_d[:, :]), writes=[b_stage1], dma="cst0")
    E("sp", lambda e: e.dma_start(out=stage2[:, :], in_=p2_d[:, :]), writes=[b_stage2], dma="cst1")
    E("sp", lambda e: e.dma_start(out=negbf[:, :], in_=bf_d[:, :]), writes=[b_const], dma="cst2")
    E("pool", lambda e: e.memset(ones_b[:, :], 1.0), writes=[b_const])
    E("pool", lambda e: e.memset(zeros_f[:, :], 0.0), writes=[b_const])
    E("pool", lambda e: e.memset(ones16[:, :], 1.0), writes=[b_const])
    E("pool", lambda e: e.memset(ident_f[:, :], 1.0), writes=[b_const])
    E("pool", lambda e: e.affine_select(out=ident_f[:, :], in_=ident_f[:, :], pattern=[[-1, 128]],
                                        compare_op=ALU.is_equal, fill=0.0, base=0, channel_multiplier=1),
      reads=[b_const], writes=[b_const])
    E("pool", lambda e: e.affine_select(out=maskneg[:, :], in_=zeros_f[:, :], pattern=[[1, 128]],
                                        compare_op=ALU.is_ge, fill=NEG, base=0, channel_multiplier=-1),
      reads=[b_const], writes=[b_const])
    E("pool", lambda e: e.tensor_copy(out=ident_b[:, :], in_=ident_f[:, :]), reads=[b_const], writes=[b_const])
    E("pool", lambda e: e.tensor_copy(out=maskneg_b[:, :], in_=maskneg[:, :]), reads=[b_const], writes=[b_const])
    E("pool", lambda e: e.memset(blk_b[:, :], 0.0), writes=[b_const])
    E("pool", lambda e: e.memset(blk_b[0:64, 0:64], 1.0), reads=[b_const], writes=[b_const])
    E("pool", lambda e: e.memset(blk_b[64:128, 64:128], 1.0), reads=[b_const], writes=[b_const])
    E("dve", lambda e: e.tensor_scalar(out=negbf[:, :], in0=negbf[:, :], scalar1=-1.0, scalar2=None, op0=ALU.mult),
      reads=[b_const], writes=[b_const])
    for c in range(NCH):
        bk, bb = ps_aux.next()
        E("pe", lambda e, c=c, bk=bk: e.transpose(out=bk[:, 0:42], in_=stage1[0:42, c * 128:(c + 1) * 128],
                                                  identity=ident_f[0:42, 0:42]),
          reads=[b_stage1, b_const], writes=[bb])
        E("dve", lambda e, c=c, bk=bk: e.tensor_copy(out=cols1[:, c, :], in_=bk[:, 0:42]), reads=[bb], writes=[b_const])
    for c in range(2 * NFC):
        bk, bb = ps_aux.next()
        E("pe", lambda e, c=c, bk=bk: e.transpose(out=bk[:, 0:8], in_=stage2[0:8, c * 128:(c + 1) * 128],
                                                  identity=ident_f[0:8, 0:8]),
          reads=[b_stage2, b_const], writes=[bb])
        E("dve", lambda e, c=c, bk=bk: e.tensor_copy(out=cols2[:, c, :], in_=bk[:, 0:8]), reads=[bb], writes=[b_const])
    E("dve", lambda e: e.tensor_scalar(out=gqs[:, :], in0=cols1[:, :, 1], scalar1=0.125, scalar2=None, op0=ALU.mult),
      reads=[b_const], writes=[b_const])

    def col1(c, r):
        return cols1[:, c, r:r + 1]

    def norm_tile(src, src_bufs, gcol, dst, dst_bufs, N, nrm):
        sq, sq_b, lnv, lnv_b, rstd, rstd_b = nrm
        ssb, ssbb = ps_aux.next()
        for c in range(NCH):
            i = c % 2
            E("act", lambda e, c=c, i=i: e.activation(out=sq[i][:, 0:N], in_=src(c), func=AF.Square),
              reads=[src_bufs[c]], writes=[sq_b[i]])
            E("pe", lambda e, c=c, i=i: e.matmul(ssb[:, 0:N], lhsT=ones_b[:, :], rhs=sq[i][:, 0:N],
                                                 start=(c == 0), stop=(c == NCH - 1)),
              reads=[sq_b[i], b_const], writes=[ssbb])
        E("act", lambda e: e.activation(out=lnv[:, 0:N], in_=ssb[:, 0:N], func=AF.Ln, bias=EPS, scale=1.0 / D),
          reads=[ssbb], writes=[lnv_b])
        E("act", lambda e: e.activation(out=rstd[:, 0:N], in_=lnv[:, 0:N], func=AF.Exp, scale=-0.5),
          reads=[lnv_b], writes=[rstd_b])
        for c in range(NCH):
            E("dve", lambda e, c=c: e.scalar_tensor_tensor(out=dst(c), in0=src(c), scalar=gcol(c), in1=rstd[:, 0:N],
                                                           op0=ALU.mult, op1=ALU.mult),
              reads=[src_bufs[c], rstd_b, b_const], writes=[dst_bufs[c]])

    def nrm_alloc(off):
        sq = [sb("sq%d_%d" % (i, off), [128, TT], BF16, off + i * TT * 2) for i in range(2)]
        lnv = sb("lnv_%d" % off, [128, TT], F32, off + 2 * KB)
        rstd = sb("rstd_%d" % off, [128, TT], F32, off + 4 * KB)
        return (sq, P.fresh(2), lnv, P.fresh(), rstd, P.fresh()), 6 * KB

    def load_x_T(b, tt, xin, xin_b, dst_fn, dst_bufs):
        i = tt % len(xin)
        E("sp", lambda e: e.dma_start(out=xin[i][:, :], in_=x_d[b, tt * 128:(tt + 1) * 128, :]),
          writes=[xin_b[i]], dma="xin%d" % i)
        for hh in range(2):
            bk, bb = ps_aux.next()
            for cc in range(4):
                c = hh * 4 + cc
                E("pe", lambda e, c=c, cc=cc, bk=bk: e.transpose(out=bk[:, cc * 128:(cc + 1) * 128],
                                                                 in_=xin[i][:, c * 128:(c + 1) * 128],
                                                                 identity=ident_f[:, :]),
                  reads=[xin_b[i], b_const], writes=[bb])
            E("dve", lambda e, hh=hh, bk=bk: e.tensor_copy(
                out=dst_fn(hh), in_=bk[:, :].rearrange("p (c t) -> p c t", c=4)),
              reads=[bb], writes=dst_bufs)

    def load_w(dst_ap, src_ap, buf, sem):
        E("pool", lambda e: e.dma_start(out=dst_ap, in_=src_ap), writes=[buf], dma=sem)

    def seq_body(b):
        o = BASE
        hT = sb("hT_f%d" % b, [128, NCH, S], BF16, o); o += 32 * KB
        oT = sb("oT_f%d" % b, [128, NCH, S], BF16, o); o += 32 * KB
        qa = [sb("qa%d_%d" % (i, b), [70, 2, S], BF16, o + i * 8 * KB) for i in range(2)]; o += 16 * KB
        ka = [sb("ka%d_%d" % (i, b), [70, 2, S], BF16, o + i * 8 * KB) for i in range(2)]; o += 16 * KB
        va = [sb("va%d_%d" % (i, b), [128, NTT, 192], BF16, o + i * 6 * KB) for i in range(2)]; o += 12 * KB
        xtmp = sb("xtmp%d" % b, [128, NCH, TT], F32, o); o += 16 * KB
        qs_full = sb("qsf%d" % b, [H, 3, S], BF16, o); o += 12 * KB
        cumt = [sb("cumt%d_%d" % (i, b), [H, TT], F32, o + i * 2 * KB) for i in range(2)]; o += 4 * KB
        et = sb("et%d" % b, [H, TT], F32, o); o += 2 * KB
        lt = sb("lt%d" % b, [H, TT], F32, o); o += 2 * KB
        r1 = sb("r1_%d" % b, [H, TT], F32, o); o += 2 * KB
        xin = [sb("xin%d_%d" % (i, b), [128, D], F32, o + i * 4 * KB) for i in range(4)]; o += 16 * KB
        PT = [sb("PT%d_%d" % (i, b), [128, TT], BF16, o + i * KB) for i in range(4)]; o += 4 * KB
        nrm, sz = nrm_alloc(o); o += sz
        rec = [sb("rec%d_%d" % (i, b), [128, TT], F32, o + i * 2 * KB) for i in range(2)]; o += 4 * KB
        wq = [sb("wq%d_%d" % (i, b), [128, NCH, 384], BF16, o + i * 6 * KB) for i in range(2)]; o += 12 * KB
        wf = sb("wf%d" % b, [128, NCH, H], BF16, o); o += 256
        assert o <= BASE + ARENA, ("fox", o)

        hT_b = [[P.fresh() for _ in range(NT)] for _ in range(NCH)]
        oT_b = [[P.fresh() for _ in range(NT)] for _ in range(NCH)]
        qa_b = P.fresh(2); ka_b = P.fresh(2); va_b = P.fresh(2)
        qaaug_b = P.fresh(2); kaaug_b = P.fresh(2)
        xtmp_b = P.fresh(NCH)
        qs_b = P.fresh(); cum_b = P.fresh(2); et_b = P.fresh(); lt_b = P.fresh(); r1_b = P.fresh()
        xin_b = P.fresh(4); PT_b = P.fresh(4); rec_b = P.fresh(2)
        wq_b = P.fresh(2); wf_b = P.fresh()

        for i in range(2):
            E("pool", lambda e, i=i: e.memset(qa[i][64:70, :, :], -1.0), writes=[qaaug_b[i]])
            E("pool", lambda e, i=i: e.memset(ka[i][64:70, :, :], 1.0), writes=[kaaug_b[i]])
            E("pool", lambda e, i=i: e.memset(va[i][:, :, 64:128], 1.0), writes=[va_b[i]])

        for t in range(NT):
            for j in range(4):
                tt = 4 * t + j
                load_x_T(b, tt, xin, xin_b,
                         lambda hh, j=j: xtmp[:, 4 * hh:4 * hh + 4, j * 128:(j + 1) * 128], xtmp_b)
            norm_tile(lambda c: xtmp[:, c, :], xtmp_b, lambda c: col1(c, 0),
                      lambda c, t=t: hT[:, c, t * TT:(t + 1) * TT], [hT_b[c][t] for c in range(NCH)], TT, nrm)

        load_w(wf[:, :, :], kview(w_in_d)[:, :, 3 * D:3 * D + H], wf_b, "wf")
        for t in range(NT):
            bk, bb = ps_aux.next()
            for kc in range(NCH):
                E("pe", lambda e, kc=kc, t=t, bk=bk: e.matmul(bk[0:H, :], lhsT=wf[:, kc, :],
                                                              rhs=hT[:, kc, t * TT:(t + 1) * TT],
                                                              start=(kc == 0), stop=(kc == NCH - 1)),
                  reads=[wf_b, hT_b[kc][t]], writes=[bb])
            E("act", lambda e, bk=bk: e.activation(out=et[:, :], in_=bk[0:H, :], func=AF.Exp, bias=negbf[:, :], scale=-1.0),
              reads=[bb, b_const], writes=[et_b])
            E("act", lambda e: e.activation(out=lt[:, :], in_=et[:, :], func=AF.Ln, bias=1.0, scale=1.0),
              reads=[et_b], writes=[lt_b])
            ci = t % 2
            init = 0.0 if t == 0 else cumt[1 - ci][:, TT - 1:TT]
            E("dve", lambda e, ci=ci, init=init: e.tensor_tensor_scan(out=cumt[ci][:, :], data0=ones16[:, :], data1=lt[:, :],
                                                                      initial=init, op0=ALU.mult, op1=ALU.subtract),
              reads=[lt_b, cum_b[1 - ci], b_const], writes=[cum_b[ci]])
            sl = slice(t * TT, (t + 1) * TT)
            E("dve", lambda e, ci=ci, sl=sl: e.tensor_copy(out=qs_full[:, 0, sl], in_=cumt[ci][:, :]),
              reads=[cum_b[ci]], writes=[qs_b])
            E("dve", lambda e, ci=ci, sl=sl: e.tensor_tensor(out=r1[:, :], in0=cumt[ci][:, :], in1=qs_full[:, 0, sl], op=ALU.subtract),
              reads=[cum_b[ci], qs_b], writes=[r1_b])
            E("dve", lambda e, sl=sl: e.tensor_copy(out=qs_full[:, 1, sl], in_=r1[:, :]), reads=[r1_b], writes=[qs_b])
            E("dve", lambda e, sl=sl: e.tensor_tensor(out=r1[:, :], in0=r1[:, :], in1=qs_full[:, 1, sl], op=ALU.subtract),
              reads=[r1_b, qs_b], writes=[r1_b])
            E("dve", lambda e, sl=sl: e.tensor_copy(out=qs_full[:, 2, sl], in_=r1[:, :]), reads=[r1_b], writes=[qs_b])

        def proj_group(g):
            i = g % 2
            for part in range(3):
                load_w(wq[i][:, :, part * 128:(part + 1) * 128],
                       kview(w_in_d)[:, :, part * D + g * 128: part * D + (g + 1) * 128], wq_b[i], "wq%d" % i)
            sq, sq_b, lnv, lnv_b, rstd, rstd_b = nrm
            si = 0
            for part, dst, dst_b in ((0, qa, qa_b), (1, ka, ka_b)):
                for t in range(NT):
                    bk, bb = ps_aux.next()
                    for kc in range(NCH):
                        E("pe", lambda e, kc=kc, t=t, bk=bk, part=part: e.matmul(
                            bk[:, :], lhsT=wq[i][:, kc, part * 128:(part + 1) * 128], rhs=hT[:, kc, t * TT:(t + 1) * TT],
                            start=(kc == 0), stop=(kc == NCH - 1)),
                          reads=[wq_b[i], hT_b[kc][t]], writes=[bb])
                    s_ = si % 2
                    si += 1
                    yield
                    yield
                    yield
                    E("act", lambda e, bk=bk, s_=s_: e.activation(out=sq[s_][:, :], in_=bk[:, :], func=AF.Square),
                      reads=[bb], writes=[sq_b[s_]])
                    yield
                    yield
                    sk, sbb = ps_aux.next()
                    E("pe", lambda e, sk=sk, s_=s_: e.matmul(sk[:, :], lhsT=blk_b[:, :], rhs=sq[s_][:, :], start=True, stop=True),
                      reads=[sq_b[s_], b_const], writes=[sbb])
                    yield
                    yield
                    E("act", lambda e, sk=sk: e.activation(out=lnv[:, :], in_=sk[:, :], func=AF.Ln, bias=EPS, scale=1.0 / DH),
                      reads=[sbb], writes=[lnv_b])
                    E("act", lambda e: e.activation(out=rstd[:, :], in_=lnv[:, :], func=AF.Exp, scale=-0.5),
                      reads=[lnv_b], writes=[rstd_b])
                    for j in range(2):
                        rows = slice(64 * j, 64 * j + 64)
                        gc = gqs[rows, g:g + 1] if part == 0 else cols1[rows, g, 2:3]
                        E("dve", lambda e, j=j, rows=rows, gc=gc, bk=bk, t=t, dst=dst: e.scalar_tensor_tensor(
                            out=dst[i][0:64, j, t * TT:(t + 1) * TT], in0=bk[rows, :], scalar=gc, in1=rstd[rows, :],
                            op0=ALU.mult, op1=ALU.mult),
                          reads=[bb, rstd_b, b_const], writes=[dst_b[i]])
                    yield
            for j in range(2):
                h = 2 * g + j
                E("sp", lambda e, j=j, h=h: e.dma_start(out=qa[i][64:67, j, :], in_=qs_full[h:h + 1, :, :]),
                  reads=[qs_b], writes=[qaaug_b[i]], dma="aug%d" % i)
                E("sp", lambda e, j=j, h=h: e.dma_start(out=ka[i][67:70, j, :], in_=qs_full[h:h + 1, :, :]),
                  reads=[qs_b], writes=[kaaug_b[i]], dma="aug%d" % i)
            for t in range(NT):
                bk, bb = ps_aux.next()
                for jj in range(4):
                    tt = 4 * t + jj
                    for kc in range(NCH):
                        E("pe", lambda e, kc=kc, tt=tt, jj=jj, bk=bk: e.matmul(
                            bk[:, jj * 128:(jj + 1) * 128], lhsT=hT[:, kc, tt * 128:(tt + 1) * 128], rhs=wq[i][:, kc, 256:384],
                            start=(kc == 0), stop=(kc == NCH - 1)),
                          reads=[wq_b[i], hT_b[kc][t]], writes=[bb])
                    if jj % 2 == 1:
                        yield
                bv = bk[:, :].rearrange("p (a c) -> p a c", a=4)
                E("dve", lambda e, t=t, bv=bv: e.tensor_copy(out=va[i][:, 4 * t:4 * t + 4, 0:64], in_=bv[:, :, 0:64]),
                  reads=[bb], writes=[va_b[i]])
                E("dve", lambda e, t=t, bv=bv: e.tensor_copy(out=va[i][:, 4 * t:4 * t + 4, 128:192], in_=bv[:, :, 64:128]),
                  reads=[bb], writes=[va_b[i]])
                yield

        pti = [0]
        reci = [0]

        LA = 2

        def attn_group(g, filler=None):
            i = g % 2
            jobs = []
            for j in range(2):
                for I in range(NT):
                    nk = 4 * I + 4
                    for kt in range(nk):
                        jobs.append((j, I, kt, nk))
            st = {}
            obs = {}

            def s_stage(n):
                j, I, kt, nk = jobs[n]
                r = kt - 4 * I
                c0 = max(0, r) * 128
                sk, sbb = ps_s.next()
                E("pe", lambda e, kt=kt, c0=c0, sk=sk, I=I, j=j: e.matmul(
                    sk[:, c0:TT], lhsT=ka[i][0:70, j, kt * 128:(kt + 1) * 128],
                    rhs=qa[i][0:70, j, I * TT + c0:(I + 1) * TT], start=True, stop=(r < 0)),
                  reads=[ka_b[i], kaaug_b[i], qa_b[i], qaaug_b[i]], writes=[sbb])
                if r >= 0:
                    E("pe", lambda e, c0=c0, sk=sk: e.matmul(sk[:, c0:c0 + 128], lhsT=ident_b[:, :], rhs=maskneg_b[:, :],
                                                            start=False, stop=True),
                      reads=[b_const], writes=[sbb])
                pi = pti[0] % len(PT)
                pti[0] += 1
                E("act", lambda e, c0=c0, sk=sk, pi=pi: e.activation(out=PT[pi][:, c0:TT], in_=sk[:, c0:TT], func=AF.Exp),
                  reads=[sbb], writes=[PT_b[pi]])
                st[n] = (pi, c0)

            def pv_stage(n):
                j, I, kt, nk = jobs[n]
                pi, c0 = st.pop(n)
                if kt == 0:
                    obs[(j, I)] = ps_o.next()
                ob, obb = obs[(j, I)]
                E("pe", lambda e, kt=kt, c0=c0, pi=pi, ob=ob, j=j, nk=nk: e.matmul(
                    ob[:, c0:TT], lhsT=va[i][:, kt, 64 * j:64 * j + 128], rhs=PT[pi][:, c0:TT],
                    start=(kt == 0), stop=(kt == nk - 1)),
                  reads=[va_b[i], PT_b[pi]], writes=[obb])
                if kt == nk - 1:
                    num = slice(64 * j, 64 * j + 64)
                    den = slice(64 * (1 - j), 64 * (1 - j) + 64)
                    ri = reci[0] % 2
                    reci[0] += 1
                    E("act", lambda e, ob=ob, den=den, ri=ri: e.activation(out=rec[ri][den, :], in_=ob[den, :], func=AF.Ln),
                      reads=[obb], writes=[rec_b[ri]])
                    E("act", lambda e, den=den, ri=ri: e.activation(out=rec[ri][den, :], in_=rec[ri][den, :], func=AF.Exp, scale=-1.0),
                      reads=[rec_b[ri]], writes=[rec_b[ri]])
                    E("dve", lambda e, ob=ob, num=num, den=den, ri=ri, I=I: e.tensor_tensor(
                        out=oT[num, g, I * TT:(I + 1) * TT], in0=ob[num, :], in1=rec[ri][den, :], op=ALU.mult),
                      reads=[obb, rec_b[ri]], writes=[oT_b[g][I]])

            N = len(jobs)
            for n in range(min(LA, N)):
                s_stage(n)
            for n in range(N):
                if n + LA < N:
                    s_stage(n + LA)
                pv_stage(n)
                if filler is not None:
                    next(filler, None)
            if filler is not None:
                for _ in filler:
                    pass

        NG = H // 2
        for _ in proj_group(0):
            pass
        for g in range(NG):
            attn_group(g, proj_group(g + 1) if g + 1 < NG else None)

        if n_phases <= 1 and False:
            pass

        xT = sb("xT%d" % b, [128, NCH, S], F32, BASE + 64 * KB)
        xT_b = [[P.fresh() for _ in range(NT)] for _ in range(NCH)]
        o2 = BASE + 128 * KB
        xin2 = [sb("xin2_%d_%d" % (i, b), [128, D], F32, o2 + i * 4 * KB) for i in range(4)]; o2 += 16 * KB
        wo2 = [sb("wo2_%d_%d" % (i, b), [128, NCH, 512], BF16, o2 + i * 8 * KB) for i in range(2)]; o2 += 16 * KB
        assert o2 <= BASE + ARENA
        xin2_b = P.fresh(4); wo2_b = P.fresh(2)
        for tt in range(NTT):
            t = tt // 4
            load_x_T(b, tt, xin2, xin2_b,
                     lambda hh, tt=tt: xT[:, 4 * hh:4 * hh + 4, tt * 128:(tt + 1) * 128],
                     [xT_b[c][t] for c in range(NCH)])
        for half in range(2):
            load_w(wo2[half][:, :, :], kview(w_o_d)[:, :, half * 512:(half + 1) * 512], wo2_b[half], "wo%d" % half)
            for t in range(NT):
                for cc in range(4):
                    c = 4 * half + cc
                    bk, bb = ps_mm.next()
                    for kc in range(NCH):
                        E("pe", lambda e, kc=kc, t=t, cc=cc, bk=bk, half=half: e.matmul(
                            bk[:, :], lhsT=wo2[half][:, kc, cc * 128:(cc + 1) * 128], rhs=oT[:, kc, t * TT:(t + 1) * TT],
                            start=(kc == 0), stop=(kc == NCH - 1)),
                          reads=[wo2_b[half], oT_b[kc][t]], writes=[bb])
                    E("dve", lambda e, c=c, t=t, bk=bk: e.tensor_tensor(out=xT[:, c, t * TT:(t + 1) * TT], in0=bk[:, :],
                                                                        in1=xT[:, c, t * TT:(t + 1) * TT], op=ALU.add),
                      reads=[bb, xT_b[c][t]], writes=[xT_b[c][t]])

        def ffn_phase(li):
            o = BASE
            hT2 = sb("hT2_%d_%d" % (li, b), [128, NCH, S], BF16, o); o += 32 * KB
            actb = sb("actb_%d_%d" % (li, b), [128, 4, S], BF16, o); o += 16 * KB
            nrm2, sz = nrm_alloc(o); o += sz
            sg = [sb("sg%d_%d_%d" % (i, li, b), [128, TT], F32, o + i * 2 * KB) for i in range(2)]; o += 4 * KB
            o = BASE + 128 * KB
            wgrp = []
            for i in range(2):
                wg_ = sb("wg%d_%d_%d" % (i, li, b), [128, NCH, 512], BF16, o); o += 8 * KB
                wv_ = sb("wv%d_%d_%d" % (i, li, b), [128, NCH, 512], BF16, o); o += 8 * KB
                wd_ = sb("wd%d_%d_%d" % (i, li, b), [128, 4, D], BF16, o); o += 8 * KB
                wgrp.append((wg_, wv_, wd_))
            ug = [sb("ug%d_%d_%d" % (i, li, b), [128, TT + 2], F32, o + i * 2080) for i in range(2)]; o += 2 * 2080
            uv = [sb("uv%d_%d_%d" % (i, li, b), [128, TT + 2], F32, o + i * 2080) for i in range(2)]; o += 2 * 2080
            ag = [sb("ag%d_%d_%d" % (i, li, b), [128, TT], F32, o + i * 2 * KB) for i in range(2)]; o += 4 * KB
            av = [sb("av%d_%d_%d" % (i, li, b), [128, TT], F32, o + i * 2 * KB) for i in range(2)]; o += 4 * KB
            assert o <= BASE + ARENA, ("ffn", o)
            hT2_b = [[P.fresh() for _ in range(NT)] for _ in range(NCH)]
            act_b = [[P.fresh() for _ in range(NT)] for _ in range(4)]
            wgrp_b = P.fresh(2)
            ug_b = P.fresh(2); uv_b = P.fresh(2); ag_b = P.fresh(2); av_b = P.fresh(2); sg_b = P.fresh(2)
            uh_b = {id(ug): P.fresh(2), id(uv): P.fresh(2)}
            for t in range(NT):
                norm_tile(lambda c, t=t: xT[:, c, t * TT:(t + 1) * TT], [xT_b[c][t] for c in range(NCH)],
                          lambda c: col1(c, 9 + li),
                          lambda c, t=t: hT2[:, c, t * TT:(t + 1) * TT], [hT2_b[c][t] for c in range(NCH)], TT, nrm2)
            groups = [(f0, min(4, NFC - f0)) for f0 in range(0, NFC, 4)]
            r0 = 4 * li
            ui = [0]

            def stage_a(wi, wg_, wv_, fc, fch, t):
                k = ui[0] % 2
                ui[0] += 1
                pk = 1 - k
                specs = ((wg_, ug, ug_b, ag, ag_b, fch), (wv_, uv, uv_b, av, av_b, NFC + fch))
                for (wsrc, ub, ub_b, acc, acc_b, chn) in specs:
                    bk, bb = ps_up.next()
                    for kc in range(NCH):
                        E("pe", lambda e, kc=kc, t=t, bk=bk, wsrc=wsrc, fc=fc: e.matmul(
                            bk[:, :], lhsT=wsrc[:, kc, fc * 128:(fc + 1) * 128], rhs=hT2[:, kc, t * TT:(t + 1) * TT],
                            start=(kc == 0), stop=(kc == NCH - 1)),
                          reads=[wgrp_b[wi], hT2_b[kc][t]], writes=[bb])
                    hb_b = uh_b[id(ub)]
                    if t == 0:
                        E("act", lambda e, ub=ub, k=k: e.activation(out=ub[k][:, 0:2], in_=zeros_f[:, 0:2], func=AF.Copy),
                          reads=[b_const], writes=[hb_b[k]])
                    else:
                        E("act", lambda e, ub=ub, k=k, pk=pk: e.activation(out=ub[k][:, 0:2], in_=ub[pk][:, TT:TT + 2], func=AF.Copy),
                          reads=[ub_b[pk]], writes=[hb_b[k]])
                    E("act", lambda e, ub=ub, k=k, bk=bk: e.activation(out=ub[k][:, 2:TT + 2], in_=bk[:, :], func=AF.Copy),
                      reads=[bb], writes=[ub_b[k]])
                    E("act", lambda e, k=k, acc=acc, chn=chn, bk=bk: e.activation(
                        out=acc[k][:, :], in_=bk[:, :], func=AF.Identity,
                        bias=cols2[:, chn, r0 + 3:r0 + 4], scale=cols2[:, chn, r0 + 2:r0 + 3]),
                      reads=[bb, b_const], writes=[acc_b[k]])
                for tap in (1, 0):
                    for (wsrc, ub, ub_b, acc, acc_b, chn) in specs:
                        E("dve", lambda e, ub=ub, k=k, acc=acc, chn=chn, tap=tap: e.scalar_tensor_tensor(
                            out=acc[k][:, :], in0=ub[k][:, tap:tap + TT], scalar=cols2[:, chn, r0 + tap:r0 + tap + 1],
                            in1=acc[k][:, :], op0=ALU.mult, op1=ALU.add),
                          reads=[ub_b[k], uh_b[id(ub)][k], acc_b[k], b_const], writes=[acc_b[k]])
                return k

            def stage_b(k, fc, t):
                E("act", lambda e, k=k: e.activation(out=sg[k][:, :], in_=ag[k][:, :], func=AF.Silu),
                  reads=[ag_b[k]], writes=[sg_b[k]])
                E("pool", lambda e, k=k, fc=fc, t=t: e.tensor_tensor(out=actb[:, fc, t * TT:(t + 1) * TT], in0=sg[k][:, :],
                                                                     in1=av[k][:, :], op=ALU.mult),
                  reads=[sg_b[k], av_b[k]], writes=[act_b[fc][t]])

            for gi, (f0, n) in enumerate(groups):
                wi = gi % 2
                wg_, wv_, wd_ = wgrp[wi]
                load_w(wg_[:, :, 0:n * 128], kview(wup_d[li])[:, :, f0 * 128:(f0 + n) * 128], wgrp_b[wi], "wg%d" % wi)
                load_w(wv_[:, :, 0:n * 128], kview(wup_d[li])[:, :, FF + f0 * 128:FF + (f0 + n) * 128], wgrp_b[wi], "wg%d" % wi)
                load_w(wd_[:, 0:n, :], wdn_d[li][f0 * 128:(f0 + n) * 128, :].rearrange("(fc p) d -> p fc d", p=128),
                       wgrp_b[wi], "wg%d" % wi)
                pend = None
                for fc in range(n):
                    fch = f0 + fc
                    for t in range(NT):
                        k = stage_a(wi, wg_, wv_, fc, fch, t)
                        if pend is not None:
                            stage_b(*pend)
                        pend = (k, fc, t)
                stage_b(*pend)
                for t in range(NT):
                    for c in range(NCH):
                        bk, bb = ps_o.next()
                        for fc in range(n):
                            E("pe", lambda e, fc=fc, t=t, c=c, bk=bk, wd_=wd_: e.matmul(
                                bk[:, :], lhsT=wd_[:, fc, c * 128:(c + 1) * 128], rhs=actb[:, fc, t * TT:(t + 1) * TT],
                                start=(fc == 0), stop=(fc == n - 1)),
                              reads=[wgrp_b[wi], act_b[fc][t]], writes=[bb])
                        E("dve", lambda e, c=c, t=t, bk=bk: e.tensor_tensor(out=xT[:, c, t * TT:(t + 1) * TT], in0=bk[:, :],
                                                                            in1=xT[:, c, t * TT:(t + 1) * TT], op=ALU.add),
                          reads=[bb, xT_b[c][t]], writes=[xT_b[c][t]])

        def conv_phase():
            o = BASE
            hT3 = sb("hT3_%d" % b, [128, NCH, S], BF16, o); o += 32 * KB
            glu = sb("glu_%d" % b, [128, NCH, S], BF16, o); o += 32 * KB
            o = BASE + 128 * KB
            cv = sb("cv_%d" % b, [128, NCH, S], BF16, o); o += 32 * KB
            diag2 = [sb("diag%d_%d" % (i, b), [128, CW, 128], BF16, o + i * CW * 256) for i in range(2)]; o += 2 * CW * 256
            nrm3, sz = nrm_alloc(o); o += sz
            sgm = sb("sgm_%d" % b, [128, TT], F32, o); o += 2 * KB
            tmpn = sb("tmpn_%d" % b, [128, TT], F32, o); o += 2 * KB
            wpa = sb("wpa_%d" % b, [128, NCH, 512], BF16, o); o += 8 * KB
            wpg = sb("wpg_%d" % b, [128, NCH, 512], BF16, BASE + 128 * KB)
            assert o <= BASE + ARENA, ("conv", o)
            hT3_b = [[P.fresh() for _ in range(NT)] for _ in range(NCH)]
            glu_b = [[P.fresh() for _ in range(NT)] for _ in range(NCH)]
            diag_bb = [[P.fresh() for _ in range(CW)] for _ in range(2)]; sgm_b = P.fresh(); tmpn_b = P.fresh(); wpa_b = P.fresh(); wpg_b = P.fresh()
            for t in range(NT):
                norm_tile(lambda c, t=t: xT[:, c, t * TT:(t + 1) * TT], [xT_b[c][t] for c in range(NCH)],
                          lambda c: col1(c, 3),
                          lambda c, t=t: hT3[:, c, t * TT:(t + 1) * TT], [hT3_b[c][t] for c in range(NCH)], TT, nrm3)
            for grp in range(2):
                load_w(wpa[:, :, :], kview(pw1_d)[:, :, grp * 512:(grp + 1) * 512], wpa_b, "wpa")
                load_w(wpg[:, :, :], kview(pw1_d)[:, :, D + grp * 512:D + (grp + 1) * 512], wpg_b, "wpg")
                for cc in range(4):
                    c = 4 * grp + cc
                    for t in range(NT):
                        ab, abb = ps_mm.next()
                        gb, gbb = ps_mm.next()
                        for (wsrc, wb, bk, bb) in ((wpa, wpa_b, ab, abb), (wpg, wpg_b, gb, gbb)):
                            for kc in range(NCH):
                                E("pe", lambda e, kc=kc, t=t, bk=bk, wsrc=wsrc, cc=cc: e.matmul(
                                    bk[:, :], lhsT=wsrc[:, kc, cc * 128:(cc + 1) * 128], rhs=hT3[:, kc, t * TT:(t + 1) * TT],
                                    start=(kc == 0), stop=(kc == NCH - 1)),
                                  reads=[wb, hT3_b[kc][t]], writes=[bb])
                        E("act", lambda e, gb=gb, c=c: e.activation(out=sgm[:, :], in_=gb[:, :], func=AF.Sigmoid,
                                                                    bias=col1(c, 5), scale=1.0),
                          reads=[gbb, b_const], writes=[sgm_b])
                        E("dve", lambda e, ab=ab, c=c, t=t: e.scalar_tensor_tensor(
                            out=glu[:, c, t * TT:(t + 1) * TT], in0=ab[:, :], scalar=col1(c, 4), in1=sgm[:, :],
                            op0=ALU.add, op1=ALU.mult),
                          reads=[abb, sgm_b, b_const], writes=[glu_b[c][t]])
            cv_b = [[P.fresh() for _ in range(NT)] for _ in range(NCH)]
            for c in range(NCH):
                diag = diag2[c % 2]
                diag_b = diag_bb[c % 2]
                for k in range(CW):
                    if k % 2 == 0:
                        E("act", lambda e, k=k, c=c, diag=diag: e.activation(out=diag[:, k, :], in_=ident_b[:, :], func=AF.Copy,
                                                                              scale=col1(c, 11 + k)),
                          reads=[b_const], writes=[diag_b[k]])
                    else:
                        E("dve", lambda e, k=k, c=c, diag=diag: e.tensor_scalar(out=diag[:, k, :], in0=ident_b[:, :],
                                                                                scalar1=col1(c, 11 + k), scalar2=None, op0=ALU.mult),
                          reads=[b_const], writes=[diag_b[k]])
                for t in range(NT):
                    bk, bb = ps_mm.next()
                    first = True
                    for k in range(CW - 1, -1, -1):
                        s = CW - 1 - k
                        lo = max(0, s - t * TT)
                        srcs = [glu_b[c][t]] + ([glu_b[c][t - 1]] if (t > 0 and s > 0) else [])
                        E("pe", lambda e, k=k, s=s, lo=lo, t=t, c=c, bk=bk, first=first, diag=diag: e.matmul(
                            bk[:, lo:TT], lhsT=diag[:, k, :], rhs=glu[:, c, t * TT + lo - s:(t + 1) * TT - s],
                            start=first, stop=(k == 0)),
                          reads=[diag_b[k]] + srcs, writes=[bb])
                        first = False
                    E("act", lambda e, c=c, t=t, bk=bk: e.activation(out=cv[:, c, t * TT:(t + 1) * TT], in_=bk[:, :],
                                                                     func=AF.Identity, bias=col1(c, 6), scale=1.0),
                      reads=[bb, b_const], writes=[cv_b[c][t]])
            cT = hT3
            cT_b = hT3_b
            sq, sq_b, lnv, lnv_b, rstd, rstd_b = nrm3
            wpb = sb("wpb_%d" % b, [128, NCH, 512], BF16, BASE + 160 * KB)
            wpb_b = P.fresh()
            wp2 = ((wpa, wpa_b, "wpa"), (wpb, wpb_b, "wpb"))
            for half in range(2):
                load_w(wp2[half][0][:, :, :], kview(pw2_d)[:, :, half * 512:(half + 1) * 512], wp2[half][1], wp2[half][2])

            def norm2(t):
                ssb, ssbb = ps_aux.next()
                for c in range(NCH):
                    i = c % 2
                    E("act", lambda e, c=c, i=i, t=t: e.activation(out=sq[i][:, :], in_=cv[:, c, t * TT:(t + 1) * TT], func=AF.Square),
                      reads=[cv_b[c][t]], writes=[sq_b[i]])
                    E("pe", lambda e, c=c, i=i, ssb=ssb: e.matmul(ssb[:, :], lhsT=ones_b[:, :], rhs=sq[i][:, :],
                                                                  start=(c == 0), stop=(c == NCH - 1)),
                      reads=[sq_b[i], b_const], writes=[ssbb])
                E("act", lambda e, ssb=ssb: e.activation(out=lnv[:, :], in_=ssb[:, :], func=AF.Ln, bias=EPS, scale=1.0 / D),
                  reads=[ssbb], writes=[lnv_b])
                E("act", lambda e: e.activation(out=rstd[:, :], in_=lnv[:, :], func=AF.Exp, scale=-0.5),
                  reads=[lnv_b], writes=[rstd_b])
                for c in range(NCH):
                    E("dve", lambda e, c=c, t=t: e.scalar_tensor_tensor(out=tmpn[:, :], in0=cv[:, c, t * TT:(t + 1) * TT],
                                                                        scalar=col1(c, 7), in1=rstd[:, :], op0=ALU.mult, op1=ALU.mult),
                      reads=[cv_b[c][t], rstd_b, b_const], writes=[tmpn_b])
                    E("act", lambda e, c=c, t=t: e.activation(out=cT[:, c, t * TT:(t + 1) * TT], in_=tmpn[:, :], func=AF.Silu),
                      reads=[tmpn_b], writes=[cT_b[c][t]])

            def pw2(t):
                for half in range(2):
                    wsrc, wb, _ = wp2[half]
                    for cc in range(4):
                        c = 4 * half + cc
                        bk, bb = ps_mm.next()
                        for kc in range(NCH):
                            E("pe", lambda e, kc=kc, t=t, cc=cc, bk=bk, wsrc=wsrc: e.matmul(
                                bk[:, :], lhsT=wsrc[:, kc, cc * 128:(cc + 1) * 128], rhs=cT[:, kc, t * TT:(t + 1) * TT],
                                start=(kc == 0), stop=(kc == NCH - 1)),
                              reads=[wb, cT_b[kc][t]], writes=[bb])
                        E("dve", lambda e, c=c, t=t, bk=bk: e.scalar_tensor_tensor(
                            out=xT[:, c, t * TT:(t + 1) * TT], in0=bk[:, :], scalar=col1(c, 8), in1=xT[:, c, t * TT:(t + 1) * TT],
                            op0=ALU.add, op1=ALU.add),
                          reads=[bb, xT_b[c][t], b_const], writes=[xT_b[c][t]])

            norm2(0)
            for t in range(NT):
                if t + 1 < NT:
                    norm2(t + 1)
                pw2(t)

        if n_phases >= 2:
            ffn_phase(0)
        if n_phases >= 3:
            conv_phase()
        if n_phases >= 4:
            ffn_phase(1)

        o = BASE
        otile = [sb("otile%d_%d" % (i, b), [128, D], F32, o + i * 4 * KB) for i in range(4)]
        otile_b = P.fresh(4)
        for tt in range(NTT):
            t = tt // 4
            i = tt % 4
            for hh in range(2):
                bk, bb = ps_aux.next()
                for cc in range(4):
                    c = 4 * hh + cc
                    E("pe", lambda e, c=c, cc=cc, bk=bk, tt=tt: e.transpose(out=bk[:, cc * 128:(cc + 1) * 128],
                                                                            in_=xT[:, c, tt * 128:(tt + 1) * 128],
                                                                            identity=ident_f[:, :]),
                      reads=[xT_b[c][t], b_const], writes=[bb])
                if hh == 0:
                    E("act", lambda e, hh=hh, bk=bk, i=i: e.activation(out=otile[i][:, hh * 512:(hh + 1) * 512], in_=bk[:, :], func=AF.Copy),
                      reads=[bb], writes=[otile_b[i]])
                else:
                    E("dve", lambda e, hh=hh, bk=bk, i=i: e.tensor_copy(out=otile[i][:, hh * 512:(hh + 1) * 512], in_=bk[:, :]),
                      reads=[bb], writes=[otile_b[i]])
            E("sp", lambda e, i=i, tt=tt: e.dma_start(out=y_d[b, tt * 128:(tt + 1) * 128, :], in_=otile[i][:, :]),
              reads=[otile_b[i]], dma="out%d" % i)

    for b_ in range(NSEQ):
        seq_body(b_)

    dma_names = sorted(P.dma_count.keys())
    final_dma = dict(P.dma_count)
    nsem_eng = {e: (P.count[e] + LIMIT - 1) // LIMIT for e in ENGS}

    import contextlib
    with contextlib.ExitStack() as st:
        sems = {}
        for e in ENGS:
            for ep in range(max(1, nsem_eng[e])):
                sems[("e", e, ep)] = st.enter_context(nc.semaphore("s_%s_%d" % (e, ep)))
        for n_ in dma_names:
            sems[("d", n_)] = st.enter_context(nc.semaphore("d_" + n_))
        block = st.enter_context(nc.Block())

        def resolve(k, v):
            if k[0] == "e":
                ep = (v - 1) // LIMIT
                return sems[("e", k[1], ep)], v - ep * LIMIT
            return sems[("d", k[1])], v

        def replay(name, eng, final=False):
            for waits, fn, tok in P.streams[name]:
                for k, v in waits:
                    s_, val = resolve(k, v)
                    eng.wait_ge(s_, val)
                inst = fn(eng)
                if tok[0][0] == "e":
                    s_, _ = resolve(tok[0], tok[1])
                    inst.then_inc(s_, 1)
                else:
                    inst.then_inc(sems[("d", tok[0][1])], 16)
            if final:
                for n_ in dma_names:
                    if n_.startswith("out"):
                        eng.wait_ge(sems[("d", n_)], final_dma[n_])

        @block.sync
        def _(e):
            replay("sp", e, final=True)

        @block.tensor
        def _(e):
            replay("pe", e)

        @block.scalar
        def _(e):
            replay("act", e)

        @block.vector
        def _(e):
            replay("dve", e)

        @block.gpsimd
        def _(e):
            replay("pool", e)

    return nc, {e: P.count[e] for e in ENGS}


def pack_params(inp):
    pvec = np.concatenate([
        inp["fox_norm_g"].reshape(1, D),
        inp["fox_q_g"].reshape(1, D),
        inp["fox_k_g"].reshape(1, D),
        inp["conv_norm_g"].reshape(1, D),
        inp["conv_b_pw1"].reshape(2, D),
        inp["conv_b_dw"].reshape(1, D),
        inp["conv_ln_g"].reshape(1, D),
        inp["conv_b_pw2"].reshape(1, D),
        inp["ffn_norm_g"].reshape(2, D),
        inp["conv_w_dw"].reshape(CW, D),
    ], axis=0).astype(np.float32)
    p2 = np.concatenate([
        inp["ffn_w_dw"][0].reshape(3, 2 * FF), inp["ffn_b_dw"][0].reshape(1, 2 * FF),
        inp["ffn_w_dw"][1].reshape(3, 2 * FF), inp["ffn_b_dw"][1].reshape(1, 2 * FF),
    ], axis=0).astype(np.float32)
    return np.ascontiguousarray(pvec), np.ascontiguousarray(p2)


_CACHE = {}


def run(inputs, n_phases=99, ncores=NCORES, max_seq_per_launch=4):
    inp = {k: np.asarray(v) for k, v in inputs.items()}
    x = inp["x"]
    B = x.shape[0]
    NSEQ_TOT = B // ncores
    NSEQ = min(NSEQ_TOT, max_seq_per_launch)
    nlaunch = NSEQ_TOT // NSEQ
    key = (NSEQ, n_phases)
    if key not in _CACHE:
        _CACHE[key] = build_program(NSEQ, n_phases)
    nc, counts = _CACHE[key]
    pvec, p2 = pack_params(inp)
    shared = {
        "pvec": pvec, "p2": p2,
        "b_f": np.ascontiguousarray(inp["fox_b_f"].reshape(H, 1).astype(np.float32)),
        "fox_w_in": np.ascontiguousarray(inp["fox_w_in"][0]),
        "fox_w_o": np.ascontiguousarray(inp["fox_w_o"][0]),
        "conv_w_pw1": np.ascontiguousarray(inp["conv_w_pw1"][0]),
        "conv_w_pw2": np.ascontiguousarray(inp["conv_w_pw2"][0]),
        "ffn_w_up0": np.ascontiguousarray(inp["ffn_w_up"][0]),
        "ffn_w_up1": np.ascontiguousarray(inp["ffn_w_up"][1]),
        "ffn_w_down0": np.ascontiguousarray(inp["ffn_w_down"][0]),
        "ffn_w_down1": np.ascontiguousarray(inp["ffn_w_down"][1]),
    }
    out = np.empty((B, S, D), np.float32)
    for l in range(nlaunch):
        in_maps = []
        for c in range(ncores):
            m = dict(shared)
            s0 = c * NSEQ_TOT + l * NSEQ
            m["x"] = np.ascontiguousarray(x[s0:s0 + NSEQ])
            in_maps.append(m)
        res = run_bass_kernel_spmd(nc, in_maps, core_ids=list(range(ncores)))
        for c in range(ncores):
            s0 = c * NSEQ_TOT + l * NSEQ
            out[s0:s0 + NSEQ] = res.results[c]["y"]
    return out


def kernel(**inputs):
    return run(inputs)
```
